# Optimizing a Trainium2 kernel written in Bass

```python
import jax
import jax.numpy as jnp
from jax import lax
import numpy as np

D_MODEL = 2048
BATCH = 4
SEQ = 2048
DEPTH = 1
DEC_BATCH = 16
DEC_SEQ = 64
PAST_LEN = 4096

CHUNK = 64
MIX_WIDTH = D_MODEL
GLA_WIDTH = MIX_WIDTH // 2
GLA_HEADS = 4
GLA_DV = GLA_WIDTH // GLA_HEADS
GLA_DK = GLA_DV // 2
GLA_GATE_RANK = 16
GLA_GATE_NORM = 16.0
GDN_WIDTH = MIX_WIDTH - GLA_WIDTH
GDN_HEADS = 8
GDN_DV = GDN_WIDTH // GDN_HEADS
GDN_DK = 128
GDN_CONV = 4
GDN_CONV_DIM = 2 * GDN_HEADS * GDN_DK + GDN_WIDTH
D_FF = 5632
FFN_CONV = 3
N_MOD = 6
EPS = 1e-6
IN_SIZES = (GLA_HEADS * GLA_DK, GLA_HEADS * GLA_DK, GLA_WIDTH, GLA_WIDTH, GLA_GATE_RANK,
            GDN_HEADS * GDN_DK, GDN_HEADS * GDN_DK, GDN_WIDTH, GDN_WIDTH, GDN_HEADS, GDN_HEADS)
D_IN = sum(IN_SIZES)
IN_OFFSETS = tuple(int(v) for v in np.cumsum(IN_SIZES)[:-1])

kernel_name = 'hymba_gla_gdn_convglu_adaln_stream'


def rmsnorm(x, gain):
    xf = x.astype(jnp.float32)
    y = xf * lax.rsqrt(jnp.mean(xf * xf, axis=-1, keepdims=True) + EPS)
    return (y * gain.astype(jnp.float32)).astype(x.dtype)


def l2norm(x):
    return x * lax.rsqrt(jnp.sum(x * x, axis=-1, keepdims=True) + EPS)


def causal_dwconv(x, buf, w):
    k_w, seq = w.shape[0], x.shape[1]
    xp = jnp.concatenate([buf.astype(x.dtype), x], axis=1)
    y = xp[:, 0:seq] * w[0]
    for i in range(1, k_w):
        y = y + xp[:, i:i + seq] * w[i]
    return y, xp[:, seq:]


def to_heads(t, n_heads):
    b, l, _ = t.shape
    return t.reshape(b, l, n_heads, -1).transpose(0, 2, 1, 3).astype(jnp.float32)


def from_heads(o, gain):
    b, h, l, d = o.shape
    o = o.transpose(0, 2, 1, 3)
    o = o * lax.rsqrt(jnp.mean(o * o, axis=-1, keepdims=True) + EPS) * gain.astype(jnp.float32)
    return o.reshape(b, l, h * d)


def blocked_recurrence(step, s0, seqs, block):
    b, h, l = seqs[0].shape[:3]
    n = l // block
    xs = tuple(jnp.moveaxis(a.reshape(b, h, n, block, a.shape[-1]), 2, 0) for a in seqs)
    s, o = lax.scan(step, s0.astype(jnp.float32), xs)
    o = jnp.moveaxis(o, 0, 2).reshape(b, h, l, o.shape[-1])
    return o, s


def gla_step(s, blk):
    q, k, v, g = blk
    c = q.shape[2]
    big_g = jnp.cumsum(g, axis=2)
    causal = jnp.tril(jnp.ones((c, c), dtype=bool))
    diff = big_g[:, :, :, None, :] - big_g[:, :, None, :, :]
    decay = jnp.exp(jnp.where(causal[:, :, None], diff, -jnp.inf))
    attn = jnp.einsum('bhid,bhjd,bhijd->bhij', q, k, decay)
    o = (jnp.einsum('bhij,bhjv->bhiv', attn, v)
         + jnp.einsum('bhid,bhdv->bhiv', q * jnp.exp(big_g), s))
    g_last = big_g[:, :, -1:, :]
    s_new = (jnp.exp(g_last[:, :, 0, :, None]) * s
             + jnp.einsum('bhjd,bhjv->bhdv', k * jnp.exp(g_last - big_g), v))
    return s_new, o


def gdn_step(s, blk):
    q, k, v, g, beta = blk
    c, dv = q.shape[2], v.shape[-1]
    big_g = jnp.cumsum(g[..., 0], axis=-1)
    causal = jnp.tril(jnp.ones((c, c), dtype=bool))
    strict = jnp.tril(jnp.ones((c, c), dtype=bool), -1)
    decay = jnp.exp(jnp.where(causal, big_g[..., :, None] - big_g[..., None, :], -jnp.inf))
    kb = k * beta
    lower = jnp.where(strict, jnp.einsum('bhid,bhjd->bhij', kb, k) * decay, 0.0)
    eye = jnp.eye(c, dtype=jnp.float32)
    rhs = jnp.concatenate([v * beta, kb * jnp.exp(big_g)[..., None]], axis=-1)
    sol = lax.linalg.triangular_solve(eye + lower, rhs, left_side=True, lower=True,
                                      unit_diagonal=True)
    u = sol[..., :dv] - jnp.einsum('bhik,bhkv->bhiv', sol[..., dv:], s)
    attn = jnp.einsum('bhid,bhjd->bhij', q, k) * decay
    o = (jnp.einsum('bhid,bhdv->bhiv', q * jnp.exp(big_g)[..., None], s)
         + jnp.einsum('bhij,bhjv->bhiv', attn, u))
    g_last = big_g[..., -1:]
    s_new = (jnp.exp(g_last)[..., None] * s
             + jnp.einsum('bhjd,bhjv->bhdv', k * jnp.exp(g_last - big_g)[..., None], u))
    return s_new, o


def token_mixers(h, st_gla, st_gdn, st_conv, w_in, gla_wg, gla_bg, gla_norm, gdn_conv_w,
                 gdn_a_log, gdn_dt_bias, gdn_norm, w_out, block):
    f32 = jnp.float32
    proj = h @ w_in
    (a_q, a_k, a_v, a_r, a_lr, b_q, b_k, b_v, b_g, b_beta, b_alpha) = jnp.split(
        proj, IN_OFFSETS, axis=-1)
    log_a = jax.nn.log_sigmoid((a_lr @ gla_wg + gla_bg).astype(f32)) / GLA_GATE_NORM
    o_a, s_gla = blocked_recurrence(
        gla_step, st_gla,
        (to_heads(a_q, GLA_HEADS) * GLA_DK ** -0.5, to_heads(a_k, GLA_HEADS),
         to_heads(a_v, GLA_HEADS), to_heads(log_a, GLA_HEADS)), block)
    o_a = from_heads(o_a, gla_norm) * jax.nn.silu(a_r.astype(f32))
    qkv, conv_new = causal_dwconv(jnp.concatenate([b_q, b_k, b_v], axis=-1), st_conv, gdn_conv_w)
    qkv = jax.nn.silu(qkv)
    c_q, c_k, c_v = jnp.split(qkv, [GDN_HEADS * GDN_DK, 2 * GDN_HEADS * GDN_DK], axis=-1)
    q = l2norm(to_heads(c_q, GDN_HEADS)) * GDN_DK ** -0.5
    k = l2norm(to_heads(c_k, GDN_HEADS))
    v = to_heads(c_v, GDN_HEADS)
    beta = jax.nn.sigmoid(to_heads(b_beta, GDN_HEADS))
    g = -jnp.exp(gdn_a_log.astype(f32))[:, None, None] * jax.nn.softplus(
        to_heads(b_alpha, GDN_HEADS) + gdn_dt_bias.astype(f32)[:, None, None])
    o_b, s_gdn = blocked_recurrence(gdn_step, st_gdn, (q, k, v, g, beta), block)
    o_b = from_heads(o_b, gdn_norm) * jax.nn.silu(b_g.astype(f32))
    out = jnp.concatenate([o_a, o_b], axis=-1).astype(h.dtype) @ w_out
    return out, s_gla, s_gdn, conv_new


def conv_ffn(h, st, w_up, ffn_conv_w, ffn_conv_b, w_down):
    gate, val = jnp.split(h @ w_up, 2, axis=-1)
    gate_c, new_st = causal_dwconv(gate, st, ffn_conv_w)
    act = jax.nn.silu(gate_c + ffn_conv_b) * val
    return act @ w_down, new_st


def trunk(x, c, states, params, block):
    (w_ada, b_ada, norm1, w_in, gla_wg, gla_bg, gla_norm, gdn_conv_w, gdn_a_log, gdn_dt_bias,
     gdn_norm, w_out, norm2, w_up, ffn_conv_w, ffn_conv_b, w_down, final_norm) = params
    gla_s, gdn_s, conv_s, ffn_s = states
    n_gla, n_gdn, n_conv, n_ffn = [], [], [], []
    for l in range(DEPTH):
        mod = jax.nn.silu(c) @ w_ada[l] + b_ada[l]
        sh1, sc1, g1, sh2, sc2, g2 = jnp.split(mod[:, None, :], N_MOD, axis=-1)
        h = rmsnorm(x, norm1[l]) * (1 + sc1) + sh1
        mix, s_a, s_b, s_c = token_mixers(
            h, gla_s[l], gdn_s[l], conv_s[l], w_in[l], gla_wg[l], gla_bg[l], gla_norm[l],
            gdn_conv_w[l], gdn_a_log[l], gdn_dt_bias[l], gdn_norm[l], w_out[l], block)
        x = x + g1 * mix
        h = rmsnorm(x, norm2[l]) * (1 + sc2) + sh2
        f, s_f = conv_ffn(h, ffn_s[l], w_up[l], ffn_conv_w[l], ffn_conv_b[l], w_down[l])
        x = x + g2 * f
        n_gla.append(s_a)
        n_gdn.append(s_b)
        n_conv.append(s_c)
        n_ffn.append(s_f)
    y = rmsnorm(x, final_norm)
    return y, jnp.stack(n_gla), jnp.stack(n_gdn), jnp.stack(n_conv), jnp.stack(n_ffn)


def setup_inputs(seed: int = 0) -> dict:
    key = jax.random.key(seed)
    ks = jax.random.split(key, 32)
    f32 = jnp.float32

    def nrm(k, shape, scale):
        return jax.random.normal(k, shape, f32) * scale

    d = D_MODEL
    dt = jnp.exp(jax.random.uniform(ks[15], (DEPTH, GDN_HEADS), f32,
                                    minval=float(np.log(1e-3)), maxval=float(np.log(1e-1))))
    return {
        'x_prompt': nrm(ks[0], (BATCH, SEQ, d), 1.0),
        'x_sample': nrm(ks[1], (DEC_BATCH, DEC_SEQ, d), 1.0),
        'c_prompt': nrm(ks[2], (BATCH, d), 1.0),
        'c_sample': nrm(ks[3], (DEC_BATCH, d), 1.0),
        'state_gla': nrm(ks[4], (DEPTH, DEC_BATCH, GLA_HEADS, GLA_DK, GLA_DV), GLA_DK ** -0.5),
        'state_gdn': nrm(ks[5], (DEPTH, DEC_BATCH, GDN_HEADS, GDN_DK, GDN_DV), GDN_DK ** -0.5),
        'state_gdn_conv': nrm(ks[6], (DEPTH, DEC_BATCH, GDN_CONV - 1, GDN_CONV_DIM), 1.0),
        'state_ffn_conv': nrm(ks[7], (DEPTH, DEC_BATCH, FFN_CONV - 1, D_FF), 1.0),
        'w_ada': nrm(ks[8], (DEPTH, d, N_MOD * d), d ** -0.5),
        'b_ada': nrm(ks[9], (DEPTH, N_MOD * d), 0.02),
        'norm1': 1.0 + nrm(ks[10], (DEPTH, d), 0.02),
        'w_in': nrm(ks[11], (DEPTH, d, D_IN), d ** -0.5),
        'gla_wg': nrm(ks[12], (DEPTH, GLA_GATE_RANK, GLA_HEADS * GLA_DK), GLA_GATE_RANK ** -0.5),
        'gla_bg': nrm(ks[13], (DEPTH, GLA_HEADS * GLA_DK), 0.02),
        'gla_norm': 1.0 + nrm(ks[14], (DEPTH, GLA_DV), 0.02),
        'gdn_conv_w': nrm(ks[16], (DEPTH, GDN_CONV, GDN_CONV_DIM), GDN_CONV ** -0.5),
        'gdn_a_log': jnp.log(jax.random.uniform(ks[17], (DEPTH, GDN_HEADS), f32,
                                                minval=1.0, maxval=16.0)),
        'gdn_dt_bias': dt + jnp.log(-jnp.expm1(-dt)),
        'gdn_norm': 1.0 + nrm(ks[18], (DEPTH, GDN_DV), 0.02),
        'w_out': nrm(ks[19], (DEPTH, MIX_WIDTH, d), MIX_WIDTH ** -0.5),
        'norm2': 1.0 + nrm(ks[20], (DEPTH, d), 0.02),
        'w_up': nrm(ks[21], (DEPTH, d, 2 * D_FF), d ** -0.5),
        'ffn_conv_w': nrm(ks[22], (DEPTH, FFN_CONV, D_FF), FFN_CONV ** -0.5),
        'ffn_conv_b': nrm(ks[23], (DEPTH, D_FF), 0.02),
        'w_down': nrm(ks[24], (DEPTH, D_FF, d), D_FF ** -0.5),
        'final_norm': 1.0 + nrm(ks[25], (d,), 0.02),
    }


def reference(x_prompt, x_sample, c_prompt, c_sample, state_gla, state_gdn, state_gdn_conv,
              state_ffn_conv, w_ada, b_ada, norm1, w_in, gla_wg, gla_bg, gla_norm, gdn_conv_w,
              gdn_a_log, gdn_dt_bias, gdn_norm, w_out, norm2, w_up, ffn_conv_w, ffn_conv_b,
              w_down, final_norm):
    params = (w_ada, b_ada, norm1, w_in, gla_wg, gla_bg, gla_norm, gdn_conv_w, gdn_a_log,
              gdn_dt_bias, gdn_norm, w_out, norm2, w_up, ffn_conv_w, ffn_conv_b, w_down,
              final_norm)
    b = x_prompt.shape[0]
    fresh = (jnp.zeros((DEPTH, b, GLA_HEADS, GLA_DK, GLA_DV), jnp.float32),
             jnp.zeros((DEPTH, b, GDN_HEADS, GDN_DK, GDN_DV), jnp.float32),
             jnp.zeros((DEPTH, b, GDN_CONV - 1, GDN_CONV_DIM), x_prompt.dtype),
             jnp.zeros((DEPTH, b, FFN_CONV - 1, D_FF), x_prompt.dtype))
    y_prompt, p_gla, p_gdn, p_conv, p_ffn = trunk(x_prompt, c_prompt, fresh, params, CHUNK)
    y_sample, s_gla, s_gdn, s_conv, s_ffn = trunk(
        x_sample, c_sample, (state_gla, state_gdn, state_gdn_conv, state_ffn_conv), params,
        x_sample.shape[1])
    return (y_prompt, y_sample, p_gla, p_gdn, p_conv, p_ffn, s_gla, s_gdn, s_conv, s_ffn)
```

```python
import numpy as np
from contextlib import ExitStack
import concourse.bass as bass
import concourse.mybir as mybir
from concourse.bass_utils import run_bass_kernel_spmd

F32 = mybir.dt.float32
BF16 = mybir.dt.bfloat16
AF = mybir.ActivationFunctionType
ALU = mybir.AluOpType

D = 2048
KD = 16
DIN = 7200
NMOD = 6
EPS = 1e-6
O_AQ, O_AK, O_AV, O_AR, O_LR = 0, 512, 1024, 2048, 3072
O_BQ, O_BK, O_BV, O_BG, O_BB = 3088, 4112, 5136, 6160, 7184
BIG = 30000.0


class Res:
    __slots__ = ("name", "w", "r")

    def __init__(self, name=""):
        self.name = name
        self.w = None
        self.r = {}


class Lane:
    def __init__(self, sem, name):
        self.sem = sem
        self.count = 0
        self.name = name


class Op:
    __slots__ = ("waits", "fn", "lane", "ninc")

    def __init__(self, waits, fn, lane, ninc=1):
        self.waits = waits
        self.fn = fn
        self.lane = lane
        self.ninc = ninc


class _Rec:
    def __init__(self):
        self.calls = []

    def __getattr__(self, name):
        def f(*a, **kw):
            self.calls.append((name, a, kw))
        return f


class Prog:
    ENG = ("pe", "act", "dve", "pool", "sp")

    def __init__(self, nc, stack):
        self.nc = nc
        self.stack = stack
        self.ops = {e: [] for e in self.ENG}
        self.seen = {e: {} for e in self.ENG}
        self.sem = {}
        for e in self.ENG:
            self.sem[e] = stack.enter_context(nc.semaphore("s_" + e))
        self.lanes = []
        self.main = []
        self.cur = self.main
        self._stack = []

    def lane(self, name):
        sem = self.stack.enter_context(self.nc.semaphore("l_" + name))
        ln = Lane(sem, name)
        self.lanes.append(ln)
        return ln

    def _collect(self, eng, reads, writes, skip_lane=None):
        need = []
        for r in reads:
            if r.w is not None:
                need.append(r.w)
        for w in writes:
            if w.w is not None:
                need.append(w.w)
            need.extend(w.r.values())
        out = {}
        seen = self.seen[eng]
        for ref in need:
            key = ref[1]
            if ref[0] == "E":
                if key == eng and eng in ("pe", "sp"):
                    continue
            elif key is skip_lane:
                continue
            val = ref[2]
            if seen.get(key, -1) >= val:
                continue
            if key not in out or out[key][2] < val:
                out[key] = ref
        for key, ref in out.items():
            seen[key] = ref[2]
        return list(out.values())

    def op(self, eng, fn, reads=(), writes=()):
        rec = _Rec()
        fn(rec)
        assert len(rec.calls) == 1
        name, a, kw = rec.calls[0]

        def fn2(e, name=name, a=a, kw=kw):
            return getattr(e, name)(*a, **kw)

        self.cur.append(("op", eng, fn2, list(reads), list(writes)))

    def dma(self, q, lane, pairs, reads=(), writes=()):
        self.cur.append(("dma", q, lane, list(pairs), list(reads), list(writes)))

    def barrier(self):
        self.cur.append(("bar",))

    def begin_stream(self):
        self._stack.append(self.cur)
        self.cur = []
        return self.cur

    def end_stream(self):
        s_ = self.cur
        self.cur = self._stack.pop()
        return s_

    def extend(self, items):
        self.cur.extend(items)

    def _schedule(self):
        for it in self.main:
            if it[0] == "op":
                self._op_now(it[1], it[2], it[3], it[4])
            elif it[0] == "dma":
                self._dma_now(it[1], it[2], it[3], it[4], it[5])
            else:
                self._barrier_now()

    def _op_now(self, eng, fn, reads=(), writes=()):
        if any(r.name.startswith("bank") for r in reads):
            writes = list(writes) + [r for r in reads if r.name.startswith("bank")]
            reads = [r for r in reads if not r.name.startswith("bank")]
        waits = self._collect(eng, reads, writes)
        idx = len(self.ops[eng])
        self.ops[eng].append(Op(waits, fn, None))
        ref = ("E", eng, idx)
        for r in reads:
            r.r[eng] = ref
        for w in writes:
            w.w = ref
            w.r = {}
        return ref

    def _dma_now(self, q, lane, pairs, reads=(), writes=()):
        waits = self._collect(q, reads, writes, skip_lane=lane)
        if lane.count and self.seen[q].get(lane, -1) < lane.count:
            waits.append(("L", lane, lane.count))
            self.seen[q][lane] = lane.count
        ref = None
        for i, (o, s) in enumerate(pairs):
            lane.count += 16
            ref = ("L", lane, lane.count)

            def fn(e, o=o, s=s):
                return e.dma_start(out=o, in_=s)

            self.ops[q].append(Op(waits if i == 0 else [], fn, lane))
        for r in reads:
            r.r[lane] = ref
        for w in writes:
            w.w = ref
            w.r = {}
        return ref

    def _barrier_now(self):
        last = {}
        for e in self.ENG:
            for idx in range(len(self.ops[e]) - 1, -1, -1):
                o = self.ops[e][idx]
                if o.lane is None and o.fn is not None:
                    last[e] = ("E", e, idx)
                    break
        lanes = [("L", ln, ln.count) for ln in self.lanes if ln.count]
        for e in self.ENG:
            waits = []
            seen = self.seen[e]
            for e2, ref in last.items():
                if e2 == e:
                    continue
                if seen.get(e2, -1) < ref[2]:
                    waits.append(ref)
                    seen[e2] = ref[2]
            for ref in lanes:
                if seen.get(ref[1], -1) < ref[2]:
                    waits.append(ref)
                    seen[ref[1]] = ref[2]
            if waits:
                self.ops[e].append(Op(waits, None, None))

    def emit(self):
        nc = self.nc
        self._schedule()
        sig = {e: set() for e in self.ENG}
        for e in self.ENG:
            for op in self.ops[e]:
                for ref in op.waits:
                    if ref[0] == "E":
                        sig[ref[1]].add(ref[2])
        tick = {e: {idx: k + 1 for k, idx in enumerate(sorted(sig[e]))} for e in self.ENG}
        final_waits = [(ln.sem, ln.count) for ln in self.lanes if ln.count]

        def run(e, eng):
            for idx, op in enumerate(self.ops[e]):
                for ref in op.waits:
                    if ref[0] == "E":
                        eng.wait_ge(self.sem[ref[1]], tick[ref[1]][ref[2]])
                    else:
                        eng.wait_ge(ref[1].sem, ref[2])
                if op.fn is None:
                    continue
                ins = op.fn(eng)
                if op.lane is not None:
                    ins.then_inc(op.lane.sem, 16)
                elif idx in tick[e]:
                    ins.then_inc(self.sem[e], 1)
            if e == "sp":
                for sem, val in final_waits:
                    eng.wait_ge(sem, val)

        with nc.Block() as block:
            block.tensor(lambda eng: run("pe", eng))
            block.scalar(lambda eng: run("act", eng))
            block.vector(lambda eng: run("dve", eng))
            block.gpsimd(lambda eng: run("pool", eng))
            block.sync(lambda eng: run("sp", eng))
        self.nticks = {e: len(tick[e]) for e in self.ENG}
        self.nops = {e: len(self.ops[e]) for e in self.ENG}


def interleave(a, b):
    out = []
    i = j = 0
    la, lb = len(a), len(b)
    while i < la or j < lb:
        if j >= lb or (i < la and i * lb <= j * la):
            out.append(a[i])
            i += 1
        else:
            out.append(b[j])
            j += 1
    return out


class Buf:
    def __init__(self, t, name):
        self.t = t
        self.name = name
        self._r = {}

    def r(self, key=0):
        if key not in self._r:
            self._r[key] = Res("%s.%s" % (self.name, key))
        return self._r[key]

    def rs(self, keys):
        return [self.r(k) for k in keys]


class Ring:
    def __init__(self, items):
        self.items = items
        self.i = 0

    def next(self):
        it = self.items[self.i % len(self.items)]
        self.i += 1
        return it


class Cfg:
    def __init__(self, npc=16, wch=15, dff=5632, gff=8):
        self.NPC = npc
        self.WCH = wch
        self.NCH = 3 + npc
        self.T = 64 * self.NCH
        self.NT = (self.NCH + 1) // 2
        self.TW = 64 * wch
        self.NTW = (wch + 1) // 2
        self.DFF = dff
        self.NFF = dff // 128
        self.GFF = gff


def host_consts():
    c = np.zeros((128, 8, 128), np.float32)
    i = np.arange(128)
    same = (i[:, None] // 64) == (i[None, :] // 64)
    c[:, 0, :] = np.eye(128)
    c[:, 1, :] = (same & (i[:, None] <= i[None, :]))
    c[:, 2, :] = same
    c[:, 3, :] = 1.0
    c[:, 4, :] = np.where(same & (i[None, :] < i[:, None]), 0.0, BIG)
    c[:, 5, :] = np.where(same & (i[:, None] <= i[None, :]), 0.0, BIG)
    c[0:64, 6, :] = 1.0
    c[64:128, 7, :] = 1.0
    return c


class Arena:
    def __init__(self, t, nwords):
        self.t = t
        self.n = nwords
        self.off = 0

    def take(self, name, shape, dt=F32):
        per = 1
        for s in shape[1:]:
            per *= s
        words = per if dt == F32 else (per + 1) // 2
        assert self.off + words <= self.n, "SBUF arena overflow at %s: %d + %d > %d" % (name, self.off, words, self.n)
        v = self.t[0:shape[0], self.off:self.off + words]
        self.off += words
        if dt != F32:
            v = v.bitcast(dt)[:, 0:per]
        if len(shape) == 3:
            v = v.rearrange("p (a b) -> p a b", a=shape[1])
        elif len(shape) == 4:
            v = v.rearrange("p (a b c) -> p a b c", a=shape[1], b=shape[2])
        return Buf(v, name)


ARENA_WORDS = 52200


class StopBuild(Exception):
    pass


STOP = [None]


def cut(k):
    if STOP[0] == k:
        raise StopBuild()


def build(cfg):
    nc = bass.Bass("TRN2", target_bir_lowering=False)
    T, NT, NCH, TW, NTW, WCH = cfg.T, cfg.NT, cfg.NCH, cfg.TW, cfg.NTW, cfg.WCH
    DFF, NFF, GFF = cfg.DFF, cfg.NFF, cfg.GFF
    TMX = max(T, TW)
    NTM = max(NT, NTW)

    def din(name, shape):
        return nc.dram_tensor(name, list(shape), F32, kind="ExternalInput").ap()

    def dout(name, shape):
        return nc.dram_tensor(name, list(shape), F32, kind="ExternalOutput").ap()

    xm = din("xm", [T, D])
    xw = din("xw", [max(TW, 64), D])
    cT_d = din("cT", [128, KD, 3])
    flag_d = din("flag", [128, 1])
    cst_d = din("cst", [128, 8, 128])
    sgla_d = din("sgla", [2, 4, 128, 256])
    sgdn_d = din("sgdn", [2, 8, 128, 128])
    sconv_d = din("sconv", [2, 24, 128, 3])
    sffn_d = din("sffn", [2, NFF, 128, 2])
    w_ada = din("w_ada", [D, NMOD * D])
    bada_fm = din("bada_fm", [128, 96])
    bada_row = din("bada_row", [1, NMOD * D])
    norm1_fm = din("norm1_fm", [128, KD])
    norm2_fm = din("norm2_fm", [128, KD])
    fnorm_row = din("fnorm_row", [1, D])
    w_in = din("w_in", [D, DIN])
    wg_d = din("gla_wg", [16, 512])
    nbg_fm = din("gla_bg_fm", [128, 4])
    glan_fm = din("gla_norm_fm", [128, 2])
    gdnn_fm = din("gdn_norm_fm", [128, 1])
    convw_fm = din("gdn_convw_fm", [128, 24, 4])
    alog_row = din("alog_row", [1, 8])
    dtb_row = din("dtb_row", [1, 8])
    w_out = din("w_out", [D, D])
    w_up = din("w_up", [D, 2 * DFF])
    fcw_fm = din("ffn_convw_fm", [128, NFF, 3])
    fcb_fm = din("ffn_convb_fm", [128, NFF])
    w_down = din("w_down", [DFF, D])

    y_d = dout("y", [T, D])
    ogla_d = dout("ogla", [3, 4, 128, 256])
    ogdn_d = dout("ogdn", [3, 8, 128, 128])
    oconv_d = dout("oconv", [3, 24, 128, 3])
    offn_d = dout("offn", [3, NFF, 128, 2])

    st = ExitStack()
    with st:
        P = Prog(nc, st)
        try:
            arena_t = st.enter_context(nc.sbuf_tensor("arena", [128, ARENA_WORDS], F32))
            AR = Arena(arena_t, ARENA_WORDS)
            sb = AR.take

            banks = [Buf(st.enter_context(nc.psum_tensor("bank%d" % i, [128, 512], F32)), "bank%d" % i)
                     for i in range(8)]
            ringT0 = Ring(banks[0:1])
            ringT1 = Ring(banks[1:2])
            ringM = Ring(banks[2:4])
            ringS = Ring(banks[5:7])
            ringC = Ring([banks[7], banks[1]])
            ringO = Ring([banks[4]])

            def ACT(fn, r=(), w=()):
                return P.op("act", fn, r, w)

            def DVE(fn, r=(), w=()):
                return P.op("dve", fn, r, w)

            def PE(fn, r=(), w=()):
                return P.op("pe", fn, r, w)

            cst = sb("cst", [128, 8, 128])
            cstb = sb("cstb", [128, 128], BF16)
            ident = cst.t[:, 0, :]
            U2 = cst.t[:, 1, :]
            B2 = cst.t[:, 2, :]
            ONES = cst.t[:, 3, :]
            BIGL = cst.t[:, 4, :]
            BIGU = cst.t[:, 5, :]
            SELA = cst.t[:, 6, :]
            SELB = cst.t[:, 7, :]
            identb = cstb.t
            RC = cst.r()
            oT = sb("oT", [128, KD, T], BF16)
            flag = sb("flag", [128, 1])
            siluT = sb("siluT", [128, KD, 3], BF16)
            cTs = sb("cTs", [128, KD, 3])
            modfm = sb("modfm", [128, 4, KD, 3])
            a1 = sb("a1", [128, 2, KD, 4])
            a2 = sb("a2", [128, 2, KD, 4])
            n1fm = sb("n1fm", [128, KD])
            n2fm = sb("n2fm", [128, KD])
            badafm = sb("badafm", [128, 96])
            sel = sb("sel", [4, 2, 128])
            wgs = sb("wgs", [16, 512])
            nbg = sb("nbg", [128, 4])
            glan = sb("glan", [128, 2])
            gdnn = sb("gdnn", [128, 1])
            convw = sb("convw", [128, 24, 4])
            fcw = sb("fcw", [128, NFF, 3])
            fcb = sb("fcb", [128, NFF])
            rowc = sb("rowc", [128, 2, 8])
            tail = sb("tail", [128, 24, 3])
            epsb = sb("epsb", [128, 1])
            stat = sb("stat", [128, 8])
            NSLOT = 3
            wsl = [sb("wsl%d" % i, [128, 4096], BF16) for i in range(NSLOT)]
            wlanes = [P.lane("w%d" % i) for i in range(NSLOT)]
            wring = Ring(list(range(NSLOT)))
            persist_end = AR.off

            def wload(parts):
                i = wring.next()
                s = wsl[i]
                pairs = [(dst(s.t), src) for (dst, src) in parts]
                P.dma("pool", wlanes[i], pairs, writes=[s.r()])
                return s

            def wview(t, nk, width, c0=0, c1=None):
                v = t[:, 0:nk * width].rearrange("p (k n) -> p k n", k=nk)
                return v[:, :, c0:(width if c1 is None else c1)]

            def wsrc(w, r0, nk, c0, c1):
                return w[r0:r0 + nk * 128, c0:c1].rearrange("(k p) n -> p k n", p=128)

            lc = P.lane("const")
            stl = Ring([P.lane("st%d" % i) for i in range(6)])
            olr = Ring([P.lane("out%d" % i) for i in range(6)])
            xlanes = [P.lane("x%d" % i) for i in range(2)]

            hT = sb("hT", [128, KD, TMX], BF16)
            xs_off = AR.off
            xs = sb("xs", [128, D])
            xnA = sb("xnA", [128, D], BF16)
            xs_end = AR.off
            TP = TMX + 16
            qraw = sb("qraw", [128, TP])
            kraw = sb("kraw", [128, TP])
            tmpA_off = AR.off
            tmpA = sb("tmpA", [128, TP])
            tmpB = sb("tmpB", [128, TMX])
            tmpB_end = AR.off
            tmpC_off = AR.off
            tmpC = sb("tmpC", [128, TMX])
            qTb = sb("qTb", [128, TMX], BF16)
            kTb = sb("kTb", [128, TMX], BF16)
            khTb = sb("khTb", [128, TMX], BF16)
            srTb = sb("srTb", [128, 2, T], BF16)
            vtb_off = AR.off
            vtb = sb("vtb", [128, NTM, 256], BF16)
            lrT = sb("lrT", [16, TMX])
            vtb_end = AR.off
            resetm = sb("resetm", [128, TMX], BF16)
            Sgla = sb("Sgla", [128, 4, 256])
            Sgdn = sb("Sgdn", [128, 8, 128])
            Sglab = sb("Sglab", [128, 4, 256], BF16)
            Sgdnb = sb("Sgdnb", [128, 8, 128], BF16)
            bag = sb("bag", [128, NTM, 16])
            gc = sb("gc", [128, 4, NTM, 8])
            dS = sb("dS", [128, 2, NTM, 8])
            eql = sb("eql", [128, TMX // 64])
            khtok = sb("khtok", [128, 2, 128], BF16)
            attn_sb = sb("attn_sb", [128, 2, 128], BF16)
            on_sb = sb("on_sb", [128, 2, 256], BF16)
            gdn0_off = AR.off
            kv_tok = sb("kv_tok", [128, 2, 2, 128], BF16)
            drv = sb("drv", [128, 2, 3, 128], BF16)
            gU = sb("gU", [128, 128])
            Dm = sb("Dm", [128, 2, 128])
            PQ = sb("PQ", [128, 2, 2, 128])
            Ym = sb("Ym", [128, 2, 128])
            TTb = sb("TTb", [128, 2, 128], BF16)
            nsk = sb("nsk", [128, 2, 128], BF16)
            erow = sb("erow", [128, 2, 128])
            at_sb = sb("at_sb", [128, 2, 128], BF16)
            qg_sb = sb("qg_sb", [128, 2, 128], BF16)
            u_sb = sb("u_sb", [128, 2, 128], BF16)
            on2 = sb("on2", [128, 2, 128], BF16)
            gdn0_end = AR.off
            scopeA_end = AR.off

            class _NS:
                pass

            GB = [_NS(), _NS()]
            for nm_ in ("kTb", "qTb", "khTb", "kv_tok", "drv", "gU", "Dm", "PQ", "Ym", "TTb", "nsk", "erow", "at_sb", "qg_sb",
                        "u_sb", "on2"):
                setattr(GB[0], nm_, locals()[nm_])
            AR.off = xs_off
            GB[1].kTb = sb("kTb1", [128, TMX], BF16)
            GB[1].qTb = sb("qTb1", [128, TMX], BF16)
            GB[1].khTb = sb("khTb1", [128, TMX], BF16)
            GB[1].PQ = sb("PQ1", [128, 2, 2, 128])
            GB[1].Ym = sb("Ym1", [128, 2, 128])
            GB[1].Dm = sb("Dm1", [128, 2, 128])
            assert AR.off <= xs_end, (AR.off, xs_end)
            yneed = 256 + 384 + 128 + 128 + 128 + 256 + 128 * 4
            if vtb_end - vtb_off >= yneed:
                AR.off = vtb_off
            else:
                AR.off = scopeA_end
            GB[1].kv_tok = sb("kv_tok1", [128, 2, 2, 128], BF16)
            GB[1].drv = sb("drv1", [128, 2, 3, 128], BF16)
            GB[1].gU = sb("gU1", [128, 128])
            GB[1].TTb = sb("TTb1", [128, 2, 128], BF16)
            GB[1].nsk = sb("nsk1", [128, 2, 128], BF16)
            GB[1].erow = sb("erow1", [128, 2, 128])
            GB[1].at_sb = sb("at_sb1", [128, 2, 128], BF16)
            GB[1].qg_sb = sb("qg_sb1", [128, 2, 128], BF16)
            GB[1].u_sb = sb("u_sb1", [128, 2, 128], BF16)
            GB[1].on2 = sb("on21", [128, 2, 128], BF16)
            if vtb_end - vtb_off >= yneed:
                assert AR.off <= vtb_end, (AR.off, vtb_end)
                AR.off = scopeA_end
            else:
                scopeA_end = AR.off
            if tmpB_end - tmpA_off >= D:
                xs2 = Buf(arena_t[:, tmpA_off:tmpA_off + D], "xs2")
            else:
                xs2 = xs
            xs_bufs = [xs, xs2]
            if TMX >= D // 2:
                xnA2 = Buf(arena_t[:, tmpC_off:tmpC_off + D // 2].bitcast(BF16), "xnA2")
            else:
                xnA2 = xnA
            xn_bufs = [xnA, xnA2]
            GA = [_NS(), _NS()]
            for nm_ in ("qTb", "kTb", "khTb", "srTb", "vtb", "khtok", "attn_sb", "on_sb", "eql"):
                setattr(GA[0], nm_, locals()[nm_])
            save_off = AR.off
            xneed = 3 * ((TMX + 1) // 2) + T
            zneed = NTM * 128 + 128 + 128 + 256 + TMX // 64 + 8
            if xs_end - xs_off >= xneed and gdn0_end - gdn0_off >= zneed:
                AR.off = xs_off
                GA[1].kTb = sb("kTb2", [128, TMX], BF16)
                GA[1].qTb = sb("qTb2", [128, TMX], BF16)
                GA[1].khTb = sb("khTb2", [128, TMX], BF16)
                GA[1].srTb = sb("srTb2", [128, 2, T], BF16)
                assert AR.off <= xs_end
                AR.off = gdn0_off
                GA[1].vtb = sb("vtb2", [128, NTM, 256], BF16)
                GA[1].khtok = sb("khtok2", [128, 2, 128], BF16)
                GA[1].attn_sb = sb("attn_sb2", [128, 2, 128], BF16)
                GA[1].on_sb = sb("on_sb2", [128, 2, 256], BF16)
                GA[1].eql = sb("eql2", [128, TMX // 64])
                assert AR.off <= gdn0_end
                AR.off = save_off
            else:
                GA[1].kTb = sb("kTb2", [128, TMX], BF16)
                GA[1].qTb = sb("qTb2", [128, TMX], BF16)
                GA[1].khTb = sb("khTb2", [128, TMX], BF16)
                GA[1].srTb = sb("srTb2", [128, 2, T], BF16)
                GA[1].vtb = sb("vtb2", [128, NTM, 256], BF16)
                GA[1].khtok = sb("khtok2", [128, 2, 128], BF16)
                GA[1].attn_sb = sb("attn_sb2", [128, 2, 128], BF16)
                GA[1].on_sb = sb("on_sb2", [128, 2, 256], BF16)
                GA[1].eql = sb("eql2", [128, TMX // 64])
                scopeA_end = AR.off
            GA[0].ringL, GA[0].ringC, GA[0].ringO, GA[0].sc = Ring([banks[5], banks[6]]), Ring([banks[7]]), Ring([banks[4]]), 2
            GA[1].ringL, GA[1].ringC, GA[1].ringO, GA[1].sc = Ring([banks[2], banks[3]]), Ring([banks[1]]), Ring([banks[0]]), 4
            GB[0].ringL, GB[0].ringC, GB[0].ringO, GB[0].j, GB[0].sc = Ring([banks[5], banks[6]]), Ring([banks[7]]), Ring([banks[4]]), 0, 4
            GB[1].ringL, GB[1].ringC, GB[1].ringO, GB[1].j, GB[1].sc = Ring([banks[2], banks[3]]), Ring([banks[1]]), Ring([banks[0]]), 1, 2

            cpairs = [
                (cst.t, cst_d), (flag.t, flag_d), (cTs.t, cT_d), (n1fm.t, norm1_fm), (n2fm.t, norm2_fm),
                (badafm.t, bada_fm), (wgs.t, wg_d), (nbg.t, nbg_fm), (glan.t, glan_fm), (gdnn.t, gdnn_fm),
                (convw.t, convw_fm), (fcw.t, fcw_fm), (fcb.t, fcb_fm),
                (rowc.t[:, 0, :], alog_row.to_broadcast([128, 8])), (rowc.t[:, 1, :], dtb_row.to_broadcast([128, 8])),
            ]
            P.dma("sp", lc, cpairs, writes=[RC])
            RCW = Res("constwork")
            DVE(lambda e: e.tensor_copy(out=cstb.t, in_=ident), [RC], [RCW])
            DVE(lambda e: e.memset(epsb.t, EPS), [], [RCW])
            DVE(lambda e: e.tensor_scalar(out=nbg.t, in0=nbg.t, scalar1=-1.0, scalar2=None, op0=ALU.mult), [RC, RCW], [RCW])
            DVE(lambda e: e.memset(sel.t, 0.0), [RCW], [RCW])
            DVE(lambda e: e.memset(resetm.t, 1.0), [RCW], [RCW])
            DVE(lambda e: e.memset(resetm.t.rearrange("p (c t) -> p c t", t=64)[:, :, 0:1], 0.0), [RCW], [RCW])
            P.dma("sp", P.lane("selc"), [
                (sel.t[0:1, 0, 0:64], cst_d[0:1, 3, 0:64]), (sel.t[1:2, 0, 64:128], cst_d[0:1, 3, 0:64]),
                (sel.t[3:4, 0, :], cst_d[0:1, 3, :]), (sel.t[2:3, 1, :], cst_d[0:1, 3, :]),
                (sel.t[3:4, 1, :], cst_d[0:1, 3, :])], reads=[], writes=[RCW])
            ACT(lambda e: e.activation(out=siluT.t, in_=cTs.t, func=AF.Silu), [RC, RCW], [RCW])
            ACT(lambda e: e.activation(out=rowc.t[:, 0, :], in_=rowc.t[:, 0, :], func=AF.Exp), [RC, RCW], [RCW])
            DVE(lambda e: e.tensor_scalar(out=rowc.t[:, 0, :], in0=rowc.t[:, 0, :], scalar1=-1.0, scalar2=None,
                                          op0=ALU.mult), [RCW], [RCW])

            def rstd_from_ss(ssv, outv, scale, res):
                ACT(lambda e: e.activation(out=outv, in_=ssv, func=AF.Ln, scale=scale, bias=epsb.t[0:ssv.shape[0], 0:1]),
                    res + [RCW], res)
                ACT(lambda e: e.activation(out=outv, in_=outv, func=AF.Exp, scale=-0.5), res, res)

            def tile_rows(t, nch):
                return 128 if 2 * t + 1 < nch else 64

            def tok_groups(tt):
                g = []
                c = 0
                while c < tt:
                    n = min(512, tt - c)
                    g.append((c, n))
                    c += n
                return g

            def norm_transpose(src_t, n, av, dstT, col0, conds, src_res, dst_res, xb):
                ss = stat.t[0:n, 0:1]
                rs = stat.t[0:n, 1:2]
                xres = [xb.r(0), xb.r(1)]
                ACT(lambda e: e.activation(out=xb.t[0:n, :], in_=src_t, func=AF.Square, accum_out=ss),
                    src_res, xres + [stat.r()])
                rstd_from_ss(ss, rs, 1.0 / D, [stat.r()])
                ACT(lambda e: e.activation(out=xb.t[0:n, :], in_=src_t, func=AF.Copy, scale=rs),
                    src_res + [stat.r()], xres)
                b0, b1 = banks[0], banks[1]
                pt = [b0.t[:].bitcast(BF16), b1.t[:].bitcast(BF16)]
                for k in range(KD):
                    bk = (b0, b1)[k // 8]
                    PE(lambda e, k=k: e.transpose(out=pt[k // 8][:, (k % 8) * 128:(k % 8) * 128 + n],
                                                  in_=xb.t[0:n, k * 128:(k + 1) * 128], identity=identb[0:n, 0:n]),
                       xres + [RCW], [bk.r()])
                for hb in range(2):
                    bk = (b0, b1)[hb]
                    pv = pt[hb].rearrange("p (k t) -> p k t", t=128)
                    for (c0, c1, cond) in conds:
                        w = c1 - c0
                        dv = dstT.t[:, hb * 8:(hb + 1) * 8, col0 + c0:col0 + c1]
                        DVE(lambda e, pv=pv, dv=dv, c0=c0, c1=c1, cond=cond, w=w, hb=hb: e.tensor_tensor(
                            out=dv, in0=pv[:, :, c0:c1],
                            in1=av.t[:, 0, hb * 8:(hb + 1) * 8, cond:cond + 1].to_broadcast([128, 8, w]), op=ALU.mult),
                            [bk.r(), av.r()], dst_res)
                        DVE(lambda e, dv=dv, cond=cond, w=w, hb=hb: e.tensor_tensor(
                            out=dv, in0=dv, in1=av.t[:, 1, hb * 8:(hb + 1) * 8, cond:cond + 1].to_broadcast([128, 8, w]),
                            op=ALU.add), [av.r()] + dst_res, dst_res)

            def main_conds(t, n):
                if t == 0:
                    return [(0, 64, 0), (64, 128, 1)]
                if t == 1:
                    return [(0, 64, 3), (64, 128, 2)]
                return [(0, n, 2)]

            def mod_fm(sec_idx, slot_idx):
                bk = ringS.next()
                for grp in range(8):
                    c0 = sec_idx * D + grp * 256
                    s = wload([(lambda t: wview(t, 16, 256), wsrc(w_ada, 0, 16, c0, c0 + 256))])
                    sv = wview(s.t, 16, 256)
                    for j in range(2):
                        ct = grp * 2 + j
                        for k in range(KD):
                            PE(lambda e, sv=sv, j=j, k=k, ct=ct: e.matmul(
                                bk.t[:, ct * 3:ct * 3 + 3], lhsT=sv[:, k, j * 128:(j + 1) * 128], rhs=siluT.t[:, k, :],
                                start=(k == 0), stop=(k == KD - 1)), [s.r(), RCW], [bk.r()])
                DVE(lambda e: e.tensor_tensor(
                    out=modfm.t[:, slot_idx, :, :], in0=bk.t[:, 0:48].rearrange("p (k c) -> p k c", c=3),
                    in1=badafm.t[:, sec_idx * 16:(sec_idx + 1) * 16].unsqueeze(2).to_broadcast([128, KD, 3]),
                    op=ALU.add), [bk.r(), RC], [modfm.r(slot_idx)])

            def mod_scale(av, nfm, sh_slot, sc_slot):
                DVE(lambda e: e.tensor_scalar(out=av.t[:, 0, :, 0:3], in0=modfm.t[:, sc_slot, :, :], scalar1=1.0,
                                              scalar2=None, op0=ALU.add), [modfm.r(sc_slot)], [av.r()])
                DVE(lambda e: e.tensor_tensor(out=av.t[:, 0, :, 0:3], in0=av.t[:, 0, :, 0:3],
                                              in1=nfm.t.unsqueeze(2).to_broadcast([128, KD, 3]), op=ALU.mult),
                    [av.r(), RC], [av.r()])
                DVE(lambda e: e.tensor_copy(out=av.t[:, 1, :, 0:3], in_=modfm.t[:, sh_slot, :, :]),
                    [modfm.r(sh_slot), av.r()], [av.r()])
                DVE(lambda e: e.tensor_scalar(out=av.t[:, :, :, 3:4], in0=av.t[:, :, :, 2:3], scalar1=flag.t[:, 0:1],
                                              scalar2=None, op0=ALU.mult), [av.r(), RC], [av.r()])

            cut(0)
            mod_fm(0, 0)
            mod_fm(1, 1)
            mod_scale(a1, n1fm, 0, 1)

            def load_x_tiles(src, nch, warm):
                ntl = (nch + 1) // 2
                for t in range(ntl):
                    n = tile_rows(t, nch)
                    xb_ = xs_bufs[t % 2]
                    P.dma("sp", xlanes[t % 2], [(xb_.t[0:n, :], src[t * 128:t * 128 + n, :])], writes=[xb_.r()])
                    conds = [(0, n, 3)] if warm else main_conds(t, n)
                    norm_transpose(xb_.t[0:n, :], n, a1, hT, t * 128, conds, [xb_.r()], [hT.r(t)], xn_bufs[t % 2])

            def chunk_layout(mode):
                if mode == "warm":
                    return WCH, [(0, WCH, None)]
                return NCH, [(0, 1, 0), (1, 1, 1), (2, NCH - 2, 2)]

            def conv_col(mode, ch):
                if mode == "warm":
                    return 3 + 64 * ch
                if ch == 0:
                    return 3
                if ch == 1:
                    return 70
                return 137 + 64 * (ch - 2)

            def ffn_col(ch):
                if ch == 0:
                    return 2
                if ch == 1:
                    return 68
                return 134 + 64 * (ch - 2)

            def chunk_runs(c0, n, colf):
                runs = []
                c = c0
                end = c0 + n
                while c < end:
                    ch = c // 64
                    ch_end = ch
                    while (ch_end + 1) * 64 < end and colf(ch_end + 1) == colf(ch_end) + 64:
                        ch_end += 1
                    ln = min(end, (ch_end + 1) * 64) - c
                    runs.append((c, ln, colf(ch) + (c - ch * 64)))
                    c += ln
                return runs

            def proj_fm(src, slot, sv_cols, tt, evac, width):
                for (c0, n) in tok_groups(tt):
                    bk = ringM.next()
                    tiles = list(range(c0 // 128, (c0 + n + 127) // 128))
                    for k in range(KD):
                        PE(lambda e, bk=bk, k=k, c0=c0, n=n: e.matmul(
                            bk.t[0:width, 0:n], lhsT=sv_cols(k), rhs=src.t[:, k, c0:c0 + n], start=(k == 0),
                            stop=(k == KD - 1)), [slot.r()] + src.rs(tiles), [bk.r()])
                    evac(bk, c0, n)

            def mixer_pass(mode):
                warm = mode == "warm"
                nch, segs = chunk_layout(mode)
                tt = 64 * nch
                ntl = (nch + 1) // 2

                torder = list(range(ntl)) if warm else list(range(1, ntl)) + [0]

                def seg_of(ch):
                    return [sg for sg in segs if sg[0] <= ch < sg[0] + sg[1]][0]

                s = wload([(lambda t: wview(t, 16, 32, 0, 16), wsrc(w_in, 0, 16, O_LR, O_LR + 16)),
                           (lambda t: wview(t, 16, 32, 16, 32), wsrc(w_in, 0, 16, O_BB, O_BB + 16))])
                sv = wview(s.t, 16, 32)
                proj_fm(hT, s, lambda k: sv[:, k, 0:16], tt,
                        lambda bk, c0, n: ACT(lambda e: e.copy(out=lrT.t[:, c0:c0 + n], in_=bk.t[0:16, 0:n]),
                                              [bk.r()], [lrT.r()]), 16)
                cut(31)
                if nch % 2 == 1:
                    DVE(lambda e: e.memset(bag.t[64:128, ntl - 1, :], 0.0), [bag.r()], [bag.r()])
                for t in range(ntl):
                    n = tile_rows(t, nch)
                    bk = ringS.next()
                    for k in range(KD):
                        PE(lambda e, bk=bk, k=k, t=t, n=n: e.matmul(bk.t[0:n, 0:16], lhsT=hT.t[:, k, t * 128:t * 128 + n],
                                                                    rhs=sv[:, k, 16:32], start=(k == 0), stop=(k == KD - 1)),
                           [s.r(), hT.r(t)], [bk.r()])
                    ACT(lambda e, bk=bk, t=t, n=n: e.copy(out=bag.t[0:n, t, :], in_=bk.t[0:n, 0:16]), [bk.r(), bag.r()], [bag.r()])
                cut(32)
                bv = bag.t[:, 0:ntl, :]
                ACT(lambda e: e.activation(out=bv[:, :, 0:8], in_=bv[:, :, 0:8], func=AF.Sigmoid), [bag.r()], [bag.r()])
                DVE(lambda e: e.tensor_tensor(out=bv[:, :, 8:16], in0=bv[:, :, 8:16],
                                              in1=rowc.t[:, 1, :].unsqueeze(1).to_broadcast([128, ntl, 8]), op=ALU.add),
                    [bag.r(), RC], [bag.r()])
                ACT(lambda e: e.activation(out=bv[:, :, 8:16], in_=bv[:, :, 8:16], func=AF.Exp), [bag.r()], [bag.r()])
                ACT(lambda e: e.activation(out=bv[:, :, 8:16], in_=bv[:, :, 8:16], func=AF.Ln, bias=1.0), [bag.r()], [bag.r()])
                DVE(lambda e: e.tensor_tensor(out=bv[:, :, 8:16], in0=bv[:, :, 8:16],
                                              in1=rowc.t[:, 0, :].unsqueeze(1).to_broadcast([128, ntl, 8]), op=ALU.mult),
                    [bag.r(), RCW], [bag.r()])
                cut(33)
                for t in range(ntl):
                    bk = ringS.next()
                    gv = bag.t[:, t, 8:16]
                    PE(lambda e, bk=bk, gv=gv: e.matmul(bk.t[:, 0:8], lhsT=U2, rhs=gv, start=True, stop=True),
                       [bag.r(), RC], [bk.r()])
                    PE(lambda e, bk=bk, gv=gv: e.matmul(bk.t[:, 8:16], lhsT=B2, rhs=gv, start=True, stop=True),
                       [bag.r(), RC], [bk.r()])
                    PE(lambda e, bk=bk, gv=gv: e.matmul(bk.t[:, 16:24], lhsT=SELA, rhs=gv, start=True, stop=True),
                       [bag.r(), RC], [bk.r()])
                    PE(lambda e, bk=bk, gv=gv: e.matmul(bk.t[:, 24:32], lhsT=SELB, rhs=gv, start=True, stop=True),
                       [bag.r(), RC], [bk.r()])
                    DVE(lambda e, bk=bk, t=t: e.tensor_copy(out=gc.t[:, 0:2, t, :],
                                                            in_=bk.t[:, 0:16].rearrange("p (a h) -> p a h", a=2)),
                        [bk.r(), gc.r()], [gc.r()])
                    ACT(lambda e, bk=bk, t=t: e.activation(out=dS.t[:, :, t, :],
                                                           in_=bk.t[:, 16:32].rearrange("p (a h) -> p a h", a=2), func=AF.Exp),
                        [bk.r(), dS.r()], [dS.r()])
                cut(34)
                g0, g1_, g2_, g3_ = (gc.t[:, i, 0:ntl, :] for i in range(4))
                DVE(lambda e: e.tensor_tensor(out=g2_, in0=g1_, in1=g0, op=ALU.subtract), [gc.r()], [gc.r()])
                ACT(lambda e: e.activation(out=g2_, in_=g2_, func=AF.Exp), [gc.r()], [gc.r()])
                ACT(lambda e: e.activation(out=g3_, in_=g0, func=AF.Exp), [gc.r()], [gc.r()])
                DVE(lambda e: e.tensor_tensor(out=g3_, in0=g3_, in1=bv[:, :, 0:8], op=ALU.mult), [gc.r(), bag.r()], [gc.r()])

                cut(3 if warm else 13)
                P.barrier()
                for hpair in range(0, 4, 2):
                    seqs = []
                    for j_ in range(2):
                        h = hpair + j_
                        A = GA[j_]
                        parts = [(lambda t: wview(t, 16, 256, 0, 128), wsrc(w_in, 0, 16, O_AK + h * 128, O_AK + (h + 1) * 128))]
                        if not warm:
                            parts.append((lambda t: wview(t, 16, 256, 128, 256),
                                          wsrc(w_in, 0, 16, O_AQ + h * 128, O_AQ + (h + 1) * 128)))
                        s1 = wload(parts)
                        s1v = wview(s1.t, 16, 256)
                        s2 = wload([(lambda t: wview(t, 16, 256), wsrc(w_in, 0, 16, O_AV + h * 256, O_AV + (h + 1) * 256))])
                        s2v = wview(s2.t, 16, 256)
                        for (c0, n) in tok_groups(tt):
                            bk = ringM.next()
                            PE(lambda e, bk=bk, c0=c0, n=n: e.matmul(bk.t[:, 0:n], lhsT=wgs.t[:, h * 128:(h + 1) * 128],
                                                                     rhs=lrT.t[:, c0:c0 + n], start=True, stop=True),
                               [lrT.r(), RC], [bk.r()])
                            ACT(lambda e, bk=bk, c0=c0, n=n: e.activation(out=tmpA.t[:, c0:c0 + n], in_=bk.t[:, 0:n], func=AF.Exp,
                                                                          scale=-1.0, bias=nbg.t[:, h:h + 1]),
                                [bk.r(), RCW], [tmpA.r()])
                        ACT(lambda e: e.activation(out=tmpA.t[:, 0:tt], in_=tmpA.t[:, 0:tt], func=AF.Ln, bias=1.0),
                            [tmpA.r()], [tmpA.r()])
                        DVE(lambda e: e.tensor_tensor_scan(out=tmpB.t[:, 0:tt], data0=resetm.t[:, 0:tt], data1=tmpA.t[:, 0:tt],
                                                           initial=0.0, op0=ALU.mult, op1=ALU.add),
                            [tmpA.r(), RCW], [tmpB.r()])
                        ACT(lambda e: e.activation(out=tmpA.t[:, 0:tt], in_=tmpB.t[:, 0:tt], func=AF.Exp, scale=-1.0 / 16),
                            [tmpB.r(), tmpA.r()], [tmpA.r()])
                        ACT(lambda e: e.activation(out=tmpC.t[:, 0:tt], in_=tmpB.t[:, 0:tt], func=AF.Exp, scale=1.0 / 16),
                            [tmpB.r()], [tmpC.r()])
                        eqv = A.eql.t[:, 0:nch]
                        DVE(lambda e: e.tensor_copy(out=eqv, in_=tmpA.t[:, 0:tt].rearrange("p (c t) -> p c t", t=64)[:, :, 63]),
                            [tmpA.r()], [A.eql.r()])
                        proj_fm(hT, s1, lambda k: s1v[:, k, 0:128], tt,
                                lambda bk, c0, n: ACT(lambda e: e.copy(out=kraw.t[:, c0:c0 + n], in_=bk.t[:, 0:n]),
                                                      [bk.r()], [kraw.r()]), 128)
                        if not warm:
                            proj_fm(hT, s1, lambda k: s1v[:, k, 128:256], tt,
                                    lambda bk, c0, n: ACT(lambda e: e.copy(out=qraw.t[:, c0:c0 + n], in_=bk.t[:, 0:n]),
                                                          [bk.r()], [qraw.r()]), 128)
                        for t in range(ntl):
                            n = tile_rows(t, nch)
                            bk = ringM.next()
                            for k in range(KD):
                                PE(lambda e, bk=bk, k=k, t=t, n=n: e.matmul(
                                    bk.t[0:n, 0:256], lhsT=hT.t[:, k, t * 128:t * 128 + n], rhs=s2v[:, k, :], start=(k == 0),
                                    stop=(k == KD - 1)), [s2.r(), hT.r(t)], [bk.r()])
                            ACT(lambda e, bk=bk, t=t, n=n: e.copy(out=A.vtb.t[0:n, t, :], in_=bk.t[0:n, 0:256]),
                                [bk.r()], [A.vtb.r(t)])
                        if not warm:
                            s3 = wload([(lambda t: wview(t, 16, 256), wsrc(w_in, 0, 16, O_AR + h * 256, O_AR + (h + 1) * 256))])
                            s3v = wview(s3.t, 16, 256)
                            for j in range(2):
                                proj_fm(hT, s3, lambda k, j=j: s3v[:, k, j * 128:(j + 1) * 128], tt,
                                        lambda bk, c0, n, j=j: ACT(lambda e: e.activation(
                                            out=A.srTb.t[:, j, c0:c0 + n], in_=bk.t[:, 0:n], func=AF.Silu), [bk.r()], [A.srTb.r()]), 128)
                        if not warm:
                            DVE(lambda e: e.scalar_tensor_tensor(out=A.qTb.t[:, 0:tt], in0=qraw.t[:, 0:tt], scalar=128.0 ** -0.5,
                                                                 in1=tmpA.t[:, 0:tt], op0=ALU.mult, op1=ALU.mult),
                                [qraw.r(), tmpA.r()], [A.qTb.r()])
                        DVE(lambda e: e.tensor_tensor(out=tmpC.t[:, 0:tt], in0=tmpC.t[:, 0:tt], in1=kraw.t[:, 0:tt], op=ALU.mult),
                            [kraw.r(), tmpC.r()], [tmpC.r()])
                        if not warm:
                            ACT(lambda e: e.copy(out=A.kTb.t[:, 0:tt], in_=tmpC.t[:, 0:tt]), [tmpC.r()], [A.kTb.r()])
                        DVE(lambda e: e.tensor_tensor(
                            out=A.khTb.t[:, 0:tt].rearrange("p (c t) -> p c t", t=64),
                            in0=tmpC.t[:, 0:tt].rearrange("p (c t) -> p c t", t=64),
                            in1=eqv.unsqueeze(2).to_broadcast([128, nch, 64]), op=ALU.mult), [tmpC.r(), A.eql.r()], [A.khTb.r()])
                        P.begin_stream()
                        Sv = Sgla.t[:, h, :]
                        Sb = Sglab.t[:, h, :]
                        RS = Sgla.r(h)
                        RSb = Sglab.r(h)
                        Ls, Cs = [], []
                        for it_, t in enumerate(torder):
                            n = tile_rows(t, nch)
                            cs = t * 128
                            par = it_ % 2
                            P.begin_stream()
                            rS, rT = A.ringL, A.ringL
                            bkT = rT.next()
                            ptv = bkT.t[:].bitcast(BF16)
                            PE(lambda e, ptv=ptv, cs=cs, n=n: e.transpose(out=ptv[0:n, 0:128], in_=A.khTb.t[:, cs:cs + n],
                                                                          identity=identb), [A.khTb.r(), RCW], [bkT.r()])
                            ACT(lambda e, ptv=ptv, n=n, par=par: e.copy(out=A.khtok.t[0:n, par, :], in_=ptv[0:n, 0:128]),
                                [bkT.r()], [A.khtok.r(par)])
                            if not warm:
                                bkA = rS.next()
                                PE(lambda e, bkA=bkA, cs=cs, n=n: e.matmul(bkA.t[0:n, 0:n], lhsT=A.kTb.t[:, cs:cs + n],
                                                                           rhs=A.qTb.t[:, cs:cs + n], start=True, stop=True),
                                   [A.kTb.r(), A.qTb.r()], [bkA.r()])
                                DVE(lambda e, bkA=bkA, n=n, par=par: e.scalar_tensor_tensor(
                                    out=A.attn_sb.t[0:n, par, 0:n], in0=BIGU[0:n, 0:n], scalar=0.0, in1=bkA.t[0:n, 0:n],
                                    op0=ALU.is_equal, op1=ALU.mult), [bkA.r(), RC], [A.attn_sb.r(par)])
                            Ls.append(P.end_stream())
                            P.begin_stream()
                            rS, rT = A.ringC, A.ringC
                            if not warm:
                                bkO = A.ringO.next()
                            for cc in range(n // 64):
                                ch = 2 * t + cc
                                p0 = cc * 64
                                seg = seg_of(ch)
                                if ch == seg[0]:
                                    if warm:
                                        DVE(lambda e: e.memset(Sv, 0.0), [RS], [RS])
                                        ACT(lambda e: e.activation(out=Sb, in_=Sv, func=AF.Copy), [RS], [RSb])
                                    elif seg[2] != 2:
                                        P.dma("sp", stl.next(), [(Sv, sgla_d[seg[2], h])], writes=[RS])
                                        ACT(lambda e: e.activation(out=Sb, in_=Sv, func=AF.Copy), [RS], [RSb])
                                if not warm:
                                    PE(lambda e, p0=p0, par=par, t=t, bkO=bkO: e.matmul(
                                        bkO.t[p0:p0 + 64, 0:256], lhsT=A.attn_sb.t[p0:p0 + 64, par, p0:p0 + 64],
                                        rhs=A.vtb.t[p0:p0 + 64, t, :], start=True, stop=False),
                                       [A.attn_sb.r(par), A.vtb.r(t)], [bkO.r()])
                                    PE(lambda e, p0=p0, cs=cs, bkO=bkO: e.matmul(
                                        bkO.t[p0:p0 + 64, 0:256], lhsT=A.qTb.t[:, cs + p0:cs + p0 + 64], rhs=Sb, start=False,
                                        stop=True), [A.qTb.r(), RSb], [bkO.r()])
                                bkS = rS.next()
                                PE(lambda e, bkS=bkS, p0=p0, par=par, t=t: e.matmul(
                                    bkS.t[:, 0:256], lhsT=A.khtok.t[p0:p0 + 64, par, :], rhs=A.vtb.t[p0:p0 + 64, t, :], start=True,
                                    stop=True), [A.khtok.r(par), A.vtb.r(t)], [bkS.r()])
                                DVE(lambda e, bkS=bkS, ch=ch: e.scalar_tensor_tensor(
                                    out=Sv, in0=Sv, scalar=A.eql.t[:, ch:ch + 1], in1=bkS.t[:, 0:256], op0=ALU.mult,
                                    op1=ALU.add), [RS, bkS.r(), A.eql.r()], [RS])
                                ACT(lambda e: e.activation(out=Sb, in_=Sv, func=AF.Copy), [RS], [RSb])
                                if (not warm) and ch == seg[0] + seg[1] - 1:
                                    P.dma("sp", olr.next(), [(ogla_d[seg[2], h], Sv)], reads=[RS])
                            if not warm:
                                ACT(lambda e, n=n, par=par, bkO=bkO: e.activation(
                                    out=A.on_sb.t[0:n, par, :], in_=bkO.t[0:n, 0:256], func=AF.Square,
                                    accum_out=stat.t[0:n, A.sc:A.sc + 1]), [bkO.r()], [A.on_sb.r(par), stat.r()])
                                rstd_from_ss(stat.t[0:n, A.sc:A.sc + 1], stat.t[0:n, A.sc + 1:A.sc + 2], 1.0 / 256, [stat.r()])
                                ACT(lambda e, n=n, par=par, bkO=bkO: e.activation(out=A.on_sb.t[0:n, par, :], in_=bkO.t[0:n, 0:256],
                                                                                  func=AF.Copy, scale=stat.t[0:n, A.sc + 1:A.sc + 2]),
                                    [bkO.r(), stat.r()], [A.on_sb.r(par)])
                                bkT2 = rT.next()
                                ptv2 = bkT2.t[:].bitcast(BF16)
                                for j in range(2):
                                    PE(lambda e, ptv2=ptv2, j=j, n=n, par=par: e.transpose(
                                        out=ptv2[:, j * 128:j * 128 + n], in_=A.on_sb.t[0:n, par, j * 128:(j + 1) * 128],
                                        identity=identb[0:n, 0:n]), [A.on_sb.r(par), RCW], [bkT2.r()])
                                for j in range(2):
                                    DVE(lambda e, ptv2=ptv2, j=j, n=n, cs=cs: e.scalar_tensor_tensor(
                                        out=oT.t[:, h * 2 + j, cs:cs + n], in0=ptv2[:, j * 128:j * 128 + n],
                                        scalar=glan.t[:, j:j + 1], in1=A.srTb.t[:, j, cs:cs + n], op0=ALU.mult, op1=ALU.mult),
                                        [bkT2.r(), A.srTb.r(), RC], [oT.r(t)])
                            Cs.append(P.end_stream())
                        P.extend(Ls[0])
                        for i_ in range(1, len(Ls)):
                            P.extend(interleave(Ls[i_], Cs[i_ - 1]))
                        P.extend(Cs[-1])
                        seqs.append(P.end_stream())
                    k_ = len(seqs[0]) // (2 * max(ntl, 1))
                    P.extend(seqs[0][:k_])
                    P.extend(interleave(seqs[0][k_:], seqs[1]))

                cut(4 if warm else 14)
                P.barrier()
                ncols = conv_col(mode, nch - 1) + 64
                colf = lambda ch: conv_col(mode, ch)
                for hpair in range(0, 8, 2):
                    seqs = []
                    for j_ in range(2):
                        h = hpair + j_
                        G = GB[j_]
                        s1 = wload([(lambda t: wview(t, 16, 256, 0, 128), wsrc(w_in, 0, 16, O_BK + h * 128, O_BK + (h + 1) * 128)),
                                    (lambda t: wview(t, 16, 256, 128, 256), wsrc(w_in, 0, 16, O_BV + h * 128, O_BV + (h + 1) * 128))])
                        s1v = wview(s1.t, 16, 256)
                        if warm:
                            s2 = wload([(lambda t: wview(t, 16, 256, 0, 128), wsrc(w_in, 0, 16, O_BQ + h * 128, O_BQ + (h + 1) * 128))])
                        else:
                            s2 = wload([(lambda t: wview(t, 16, 256, 0, 128), wsrc(w_in, 0, 16, O_BQ + h * 128, O_BQ + (h + 1) * 128)),
                                        (lambda t: wview(t, 16, 256, 128, 256), wsrc(w_in, 0, 16, O_BG + h * 128, O_BG + (h + 1) * 128))])
                        s2v = wview(s2.t, 16, 256)

                        def evac_conv(raw):
                            def f(bk, c0, n):
                                for (c, ln, d0) in chunk_runs(c0, n, colf):
                                    ACT(lambda e, c=c, ln=ln, d0=d0: e.copy(out=raw.t[:, d0:d0 + ln], in_=bk.t[:, c - c0:c - c0 + ln]),
                                        [bk.r()], [raw.r()])
                            return f

                        def conv_silu(raw, ci, dst):
                            for (c_first, n_c, slot) in segs:
                                d0 = colf(c_first) - 3
                                if warm:
                                    DVE(lambda e, d0=d0: e.memset(raw.t[:, d0:d0 + 3], 0.0), [raw.r()], [raw.r()])
                                elif slot == 2:
                                    DVE(lambda e, d0=d0: e.tensor_copy(out=raw.t[:, d0:d0 + 3], in_=tail.t[:, ci, :]),
                                        [raw.r(), tail.r()], [raw.r()])
                                else:
                                    P.dma("sp", stl.next(), [(raw.t[:, d0:d0 + 3], sconv_d[slot, ci])], writes=[raw.r()])
                            w = ncols - 3
                            DVE(lambda e: e.tensor_scalar(out=dst.t[:, 3:3 + w], in0=raw.t[:, 3:3 + w],
                                                          scalar1=convw.t[:, ci, 3:4], scalar2=None, op0=ALU.mult),
                                [raw.r(), RC, dst.r()], [dst.r()])
                            for i in range(3):
                                DVE(lambda e, i=i: e.scalar_tensor_tensor(
                                    out=dst.t[:, 3:3 + w], in0=raw.t[:, i:i + w], scalar=convw.t[:, ci, i:i + 1],
                                    in1=dst.t[:, 3:3 + w], op0=ALU.mult, op1=ALU.add), [raw.r(), dst.r(), RC], [dst.r()])
                            for (c_first, n_c, slot) in segs:
                                e0 = colf(c_first + n_c - 1) + 64
                                if warm:
                                    DVE(lambda e, e0=e0: e.tensor_copy(out=tail.t[:, ci, :], in_=raw.t[:, e0 - 3:e0]),
                                        [raw.r(), tail.r()], [tail.r()])
                                else:
                                    P.dma("sp", olr.next(), [(oconv_d[slot, ci], raw.t[:, e0 - 3:e0])], reads=[raw.r()])

                        def compact_silu(srcb, dst):
                            for (c_first, n_c, slot) in segs:
                                d0 = colf(c_first)
                                ACT(lambda e, d0=d0, c_first=c_first, n_c=n_c: e.activation(
                                    out=dst.t[:, c_first * 64:(c_first + n_c) * 64], in_=srcb.t[:, d0:d0 + n_c * 64],
                                    func=AF.Silu), [srcb.r(), dst.r()], [dst.r()])

                        def l2n(src, dstb, scale):
                            DVE(lambda e: e.tensor_tensor(out=tmpC.t[:, 0:tt], in0=src.t[:, 0:tt], in1=src.t[:, 0:tt], op=ALU.mult),
                                [src.r(), tmpC.r()], [tmpC.r()])
                            for (c0, n) in tok_groups(tt):
                                bk = ringM.next()
                                PE(lambda e, bk=bk, c0=c0, n=n: e.matmul(bk.t[:, 0:n], lhsT=ONES, rhs=tmpC.t[:, c0:c0 + n],
                                                                         start=True, stop=True), [tmpC.r(), RC], [bk.r()])
                                ACT(lambda e, bk=bk, c0=c0, n=n: e.activation(out=tmpC.t[:, c0:c0 + n], in_=bk.t[:, 0:n],
                                                                              func=AF.Ln, bias=epsb.t[:, 0:1], scale=1.0),
                                    [bk.r(), tmpC.r(), RCW], [tmpC.r()])
                            ACT(lambda e: e.activation(out=tmpC.t[:, 0:tt], in_=tmpC.t[:, 0:tt], func=AF.Exp, scale=-0.5),
                                [tmpC.r()], [tmpC.r()])
                            DVE(lambda e: e.scalar_tensor_tensor(out=dstb.t[:, 0:tt], in0=src.t[:, 0:tt], scalar=scale,
                                                                 in1=tmpC.t[:, 0:tt], op0=ALU.mult, op1=ALU.mult),
                                [src.r(), tmpC.r()], [dstb.r()])

                        proj_fm(hT, s1, lambda k: s1v[:, k, 0:128], tt, evac_conv(kraw), 128)
                        proj_fm(hT, s1, lambda k: s1v[:, k, 128:256], tt, evac_conv(qraw), 128)
                        conv_silu(kraw, 8 + h, tmpA)
                        compact_silu(tmpA, tmpB)
                        l2n(tmpB, G.kTb, 1.0)
                        cut(41)
                        conv_silu(qraw, 16 + h, tmpA)
                        compact_silu(tmpA, G.khTb)
                        cut(42)
                        if not warm:
                            proj_fm(hT, s2, lambda k: s2v[:, k, 0:128], tt, evac_conv(kraw), 128)
                            proj_fm(hT, s2, lambda k: s2v[:, k, 128:256], tt,
                                    lambda bk, c0, n: ACT(lambda e: e.activation(out=srTb.t[:, G.j, c0:c0 + n], in_=bk.t[:, 0:n],
                                                                                 func=AF.Silu), [bk.r()], [srTb.r()]), 128)
                            conv_silu(kraw, h, tmpA)
                            compact_silu(tmpA, tmpB)
                            l2n(tmpB, G.qTb, 128.0 ** -0.5)
                        else:
                            lt = ntl - 1
                            n = tile_rows(lt, nch)
                            bk = ringM.next()
                            for k in range(KD):
                                PE(lambda e, bk=bk, k=k, lt=lt, n=n: e.matmul(bk.t[:, 0:n], lhsT=s2v[:, k, 0:128],
                                                                              rhs=hT.t[:, k, lt * 128:lt * 128 + n], start=(k == 0),
                                                                              stop=(k == KD - 1)), [s2.r(), hT.r(lt)], [bk.r()])
                            ACT(lambda e, bk=bk, n=n: e.copy(out=tail.t[:, h, :], in_=bk.t[:, n - 3:n]), [bk.r(), tail.r()], [tail.r()])

                        cut(43)
                        P.begin_stream()
                        Sv = Sgdn.t[:, h, :]
                        Sb = Sgdnb.t[:, h, :]
                        RS = Sgdn.r(h)
                        RSb = Sgdnb.r(h)
                        vTb = G.khTb
                        Ls, Cs = [], []
                        for it_, t in enumerate(torder):
                            n = tile_rows(t, nch)
                            cs = t * 128
                            par = it_ % 2
                            P.begin_stream()
                            rS, rT = G.ringL, G.ringL
                            bkT = rT.next()
                            ptv = bkT.t[:].bitcast(BF16)
                            PE(lambda e, ptv=ptv, cs=cs, n=n: e.transpose(out=ptv[0:n, 0:128], in_=G.kTb.t[:, cs:cs + n], identity=identb),
                               [G.kTb.r(), RCW], [bkT.r()])
                            PE(lambda e, ptv=ptv, cs=cs, n=n: e.transpose(out=ptv[0:n, 128:256], in_=vTb.t[:, cs:cs + n], identity=identb),
                               [vTb.r(), RCW], [bkT.r()])
                            ACT(lambda e, ptv=ptv, n=n, par=par: e.copy(out=G.kv_tok.t[0:n, par, :, :].rearrange("p a d -> p (a d)"),
                                                                        in_=ptv[0:n, 0:256]), [bkT.r()], [G.kv_tok.r(par)])
                            DVE(lambda e, n=n, par=par, t=t: e.tensor_scalar(out=G.drv.t[0:n, par, 0, :], in0=G.kv_tok.t[0:n, par, 0, :],
                                                                             scalar1=gc.t[0:n, 2, t, h:h + 1], scalar2=None, op0=ALU.mult),
                                [G.kv_tok.r(par), gc.r()], [G.drv.r(par)])
                            DVE(lambda e, n=n, par=par, t=t: e.tensor_scalar(out=G.drv.t[0:n, par, 1, :], in0=G.kv_tok.t[0:n, par, 1, :],
                                                                             scalar1=bag.t[0:n, t, h:h + 1], scalar2=None, op0=ALU.mult),
                                [G.kv_tok.r(par), bag.r(), G.drv.r(par)], [G.drv.r(par)])
                            DVE(lambda e, n=n, par=par, t=t: e.tensor_scalar(out=G.drv.t[0:n, par, 2, :], in0=G.kv_tok.t[0:n, par, 0, :],
                                                                             scalar1=gc.t[0:n, 3, t, h:h + 1], scalar2=None, op0=ALU.mult),
                                [G.kv_tok.r(par), gc.r(), G.drv.r(par)], [G.drv.r(par)])
                            cut(44)
                            DVE(lambda e, n=n, t=t: e.tensor_scalar(out=G.gU.t[0:n, 0:n], in0=U2[0:n, 0:n], scalar1=bag.t[0:n, t, 8 + h:9 + h],
                                                                    scalar2=None, op0=ALU.mult), [bag.r(), RC, G.gU.r()], [G.gU.r()])
                            bkG = rS.next()
                            PE(lambda e, bkG=bkG, n=n: e.matmul(bkG.t[:, 0:n], lhsT=ONES[0:n, :], rhs=G.gU.t[0:n, 0:n], start=True,
                                                                stop=True), [G.gU.r(), RC], [bkG.r()])
                            DVE(lambda e, bkG=bkG, n=n, t=t: e.scalar_tensor_tensor(
                                out=G.Dm.t[0:n, 0, 0:n], in0=bkG.t[0:n, 0:n], scalar=gc.t[0:n, 0, t, h:h + 1], in1=BIGL[0:n, 0:n],
                                op0=ALU.subtract, op1=ALU.max), [bkG.r(), gc.r(), RC, G.Dm.r()], [G.Dm.r()])
                            DVE(lambda e, bkG=bkG, n=n, t=t: e.tensor_scalar(
                                out=G.Dm.t[0:n, 1, 0:n], in0=bkG.t[0:n, 0:n], scalar1=gc.t[0:n, 0, t, h:h + 1], scalar2=-1.0,
                                op0=ALU.subtract, op1=ALU.mult), [bkG.r(), gc.r(), G.Dm.r()], [G.Dm.r()])
                            DVE(lambda e, n=n: e.tensor_tensor(out=G.Dm.t[0:n, 1, 0:n], in0=G.Dm.t[0:n, 1, 0:n], in1=BIGU[0:n, 0:n],
                                                               op=ALU.max), [G.Dm.r(), RC], [G.Dm.r()])
                            ACT(lambda e, n=n: e.activation(out=G.Dm.t[0:n, :, 0:n], in_=G.Dm.t[0:n, :, 0:n], func=AF.Exp, scale=-1.0),
                                [G.Dm.r()], [G.Dm.r()])
                            if not warm:
                                ACT(lambda e, bkG=bkG, n=n, par=par: e.activation(out=G.erow.t[:, par, 0:n], in_=bkG.t[:, 0:n], func=AF.Exp),
                                    [bkG.r(), G.erow.r(par)], [G.erow.r(par)])
                                DVE(lambda e, n=n, par=par, cs=cs: e.tensor_tensor(out=G.qg_sb.t[:, par, 0:n], in0=G.qTb.t[:, cs:cs + n],
                                                                                  in1=G.erow.t[:, par, 0:n], op=ALU.mult),
                                    [G.qTb.r(), G.erow.r(par)], [G.qg_sb.r(par)])
                            cut(45)
                            bkA = rS.next()
                            PE(lambda e, bkA=bkA, cs=cs, n=n: e.matmul(bkA.t[0:n, 0:n], lhsT=G.kTb.t[:, cs:cs + n], rhs=G.kTb.t[:, cs:cs + n],
                                                                       start=True, stop=True), [G.kTb.r()], [bkA.r()])
                            if not warm:
                                PE(lambda e, bkA=bkA, cs=cs, n=n: e.matmul(bkA.t[0:n, 128:128 + n], lhsT=G.kTb.t[:, cs:cs + n],
                                                                           rhs=G.qTb.t[:, cs:cs + n], start=True, stop=True),
                                   [G.kTb.r(), G.qTb.r()], [bkA.r()])
                            DVE(lambda e, bkA=bkA, n=n, t=t: e.scalar_tensor_tensor(
                                out=G.PQ.t[0:n, 0, 0, 0:n], in0=bkA.t[0:n, 0:n], scalar=bag.t[0:n, t, h:h + 1], in1=G.Dm.t[0:n, 0, 0:n],
                                op0=ALU.mult, op1=ALU.mult), [bkA.r(), bag.r(), G.Dm.r(), G.PQ.r()], [G.PQ.r()])
                            if not warm:
                                DVE(lambda e, bkA=bkA, n=n, par=par: e.tensor_tensor(out=G.at_sb.t[0:n, par, 0:n], in0=bkA.t[0:n, 128:128 + n],
                                                                                    in1=G.Dm.t[0:n, 1, 0:n], op=ALU.mult),
                                    [bkA.r(), G.Dm.r()], [G.at_sb.r(par)])
                            bkX = rS.next()
                            PE(lambda e, bkX=bkX, n=n: e.matmul(bkX.t[0:n, 0:n], lhsT=G.PQ.t[0:n, 0, 0, 0:n], rhs=ident[0:n, 0:n], start=True, stop=True),
                               [G.PQ.r(), RC], [bkX.r()])
                            ACT(lambda e, bkX=bkX, n=n: e.copy(out=G.PQ.t[0:n, 0, 1, 0:n], in_=bkX.t[0:n, 0:n]), [bkX.r(), G.PQ.r()], [G.PQ.r()])
                            DVE(lambda e, bkX=bkX, n=n: e.tensor_tensor(out=G.Ym.t[0:n, 0, 0:n], in0=ident[0:n, 0:n], in1=bkX.t[0:n, 0:n],
                                                                        op=ALU.subtract), [bkX.r(), RC, G.Ym.r()], [G.Ym.r()])
                            cut(46)
                            cur = 0
                            for it in range(5):
                                nxt = 1 - cur
                                Pc, Qc = G.PQ.t[0:n, cur, 0, 0:n], G.PQ.t[0:n, cur, 1, 0:n]
                                Pn, Qn = G.PQ.t[0:n, nxt, 0, 0:n], G.PQ.t[0:n, nxt, 1, 0:n]
                                bkP = rS.next()
                                PE(lambda e, bkP=bkP, Pc=Pc, Qc=Qc, n=n: e.matmul(bkP.t[0:n, 0:n], lhsT=Qc, rhs=Pc, start=True, stop=True),
                                   [G.PQ.r()], [bkP.r()])
                                if it < 4:
                                    PE(lambda e, bkP=bkP, Pc=Pc, Qc=Qc, n=n: e.matmul(bkP.t[0:n, 128:128 + n], lhsT=Pc, rhs=Qc, start=True,
                                                                                     stop=True), [G.PQ.r()], [bkP.r()])
                                if it < 4:
                                    ACT(lambda e, bkP=bkP, nxt=nxt, n=n: e.copy(
                                        out=G.PQ.t[0:n, nxt, :, 0:n], in_=bkP.t[0:n, 0:256].rearrange("p (a b) -> p a b", a=2)[:, :, 0:n]),
                                        [bkP.r(), G.PQ.r()], [G.PQ.r()])
                                else:
                                    ACT(lambda e, bkP=bkP, Pn=Pn, n=n: e.copy(out=Pn, in_=bkP.t[0:n, 0:n]), [bkP.r(), G.PQ.r()], [G.PQ.r()])
                                bkY = rS.next()
                                yc, yn = G.Ym.t[0:n, cur, 0:n], G.Ym.t[0:n, nxt, 0:n]
                                PE(lambda e, bkY=bkY, Pn=Pn, yc=yc, n=n: e.matmul(bkY.t[0:n, 0:n], lhsT=Pn, rhs=yc, start=True, stop=True),
                                   [G.PQ.r(), G.Ym.r()], [bkY.r()])
                                DVE(lambda e, bkY=bkY, yc=yc, yn=yn, n=n: e.tensor_tensor(out=yn, in0=bkY.t[0:n, 0:n], in1=yc, op=ALU.add),
                                    [bkY.r(), G.Ym.r()], [G.Ym.r()])
                                cur = nxt
                            ACT(lambda e, n=n, par=par, cur=cur: e.copy(out=G.TTb.t[0:n, par, 0:n], in_=G.Ym.t[0:n, cur, 0:n]),
                                [G.Ym.r()], [G.TTb.r(par)])
                            cut(47)
                            bkK = rS.next()
                            PE(lambda e, bkK=bkK, n=n, par=par: e.matmul(bkK.t[:, 0:n], lhsT=G.drv.t[0:n, par, 2, :], rhs=G.TTb.t[0:n, par, 0:n],
                                                                         start=True, stop=True), [G.drv.r(par), G.TTb.r(par)], [bkK.r()])
                            ACT(lambda e, bkK=bkK, n=n, par=par: e.activation(out=G.nsk.t[:, par, 0:n], in_=bkK.t[:, 0:n], func=AF.Copy,
                                                                              scale=-1.0), [bkK.r()], [G.nsk.r(par)])
                            Ls.append(P.end_stream())
                            P.begin_stream()
                            rS, rT = G.ringC, G.ringC
                            if not warm:
                                bkO = G.ringO.next()
                            for cc in range(n // 64):
                                ch = 2 * t + cc
                                p0 = cc * 64
                                seg = seg_of(ch)
                                if ch == seg[0]:
                                    if warm:
                                        DVE(lambda e: e.memset(Sv, 0.0), [RS], [RS])
                                        ACT(lambda e: e.activation(out=Sb, in_=Sv, func=AF.Copy), [RS], [RSb])
                                    elif seg[2] != 2:
                                        P.dma("sp", stl.next(), [(Sv, sgdn_d[seg[2], h])], writes=[RS])
                                        ACT(lambda e: e.activation(out=Sb, in_=Sv, func=AF.Copy), [RS], [RSb])
                                bkU = rS.next()
                                PE(lambda e, bkU=bkU, p0=p0, par=par: e.matmul(
                                    bkU.t[p0:p0 + 64, 0:128], lhsT=G.TTb.t[p0:p0 + 64, par, p0:p0 + 64], rhs=G.drv.t[p0:p0 + 64, par, 1, :],
                                    start=True, stop=False), [G.TTb.r(par), G.drv.r(par)], [bkU.r()])
                                PE(lambda e, bkU=bkU, p0=p0, par=par: e.matmul(
                                    bkU.t[p0:p0 + 64, 0:128], lhsT=G.nsk.t[:, par, p0:p0 + 64], rhs=Sb, start=False, stop=True),
                                   [G.nsk.r(par), RSb], [bkU.r()])
                                ACT(lambda e, bkU=bkU, p0=p0, par=par: e.copy(out=G.u_sb.t[p0:p0 + 64, par, :], in_=bkU.t[p0:p0 + 64, 0:128]),
                                    [bkU.r()], [G.u_sb.r((par, cc))])
                                if not warm:
                                    PE(lambda e, p0=p0, par=par, bkO=bkO: e.matmul(
                                        bkO.t[p0:p0 + 64, 0:128], lhsT=G.qg_sb.t[:, par, p0:p0 + 64], rhs=Sb, start=True, stop=False),
                                       [G.qg_sb.r(par), RSb], [bkO.r()])
                                    PE(lambda e, p0=p0, par=par, bkO=bkO: e.matmul(
                                        bkO.t[p0:p0 + 64, 0:128], lhsT=G.at_sb.t[p0:p0 + 64, par, p0:p0 + 64], rhs=G.u_sb.t[p0:p0 + 64, par, :],
                                        start=False, stop=True), [G.at_sb.r(par), G.u_sb.r((par, cc))], [bkO.r()])
                                bkS = rS.next()
                                PE(lambda e, bkS=bkS, p0=p0, par=par: e.matmul(
                                    bkS.t[:, 0:128], lhsT=G.drv.t[p0:p0 + 64, par, 0, :], rhs=G.u_sb.t[p0:p0 + 64, par, :], start=True, stop=True),
                                   [G.drv.r(par), G.u_sb.r((par, cc))], [bkS.r()])
                                DVE(lambda e, bkS=bkS, cc=cc, t=t: e.scalar_tensor_tensor(
                                    out=Sv, in0=Sv, scalar=dS.t[:, cc, t, h:h + 1], in1=bkS.t[:, 0:128], op0=ALU.mult, op1=ALU.add),
                                    [RS, bkS.r(), dS.r()], [RS])
                                ACT(lambda e: e.activation(out=Sb, in_=Sv, func=AF.Copy), [RS], [RSb])
                                if (not warm) and ch == seg[0] + seg[1] - 1:
                                    P.dma("sp", olr.next(), [(ogdn_d[seg[2], h], Sv)], reads=[RS])
                            if not warm:
                                ACT(lambda e, n=n, par=par, bkO=bkO: e.activation(out=G.on2.t[0:n, par, :], in_=bkO.t[0:n, 0:128], func=AF.Square,
                                                                                  accum_out=stat.t[0:n, G.sc:G.sc + 1]), [bkO.r()], [G.on2.r(par), stat.r()])
                                rstd_from_ss(stat.t[0:n, G.sc:G.sc + 1], stat.t[0:n, G.sc + 1:G.sc + 2], 1.0 / 128, [stat.r()])
                                ACT(lambda e, n=n, par=par, bkO=bkO: e.activation(out=G.on2.t[0:n, par, :], in_=bkO.t[0:n, 0:128], func=AF.Copy,
                                                                                  scale=stat.t[0:n, G.sc + 1:G.sc + 2]), [bkO.r(), stat.r()], [G.on2.r(par)])
                                bkT2 = rT.next()
                                ptv2 = bkT2.t[:].bitcast(BF16)
                                PE(lambda e, ptv2=ptv2, n=n, par=par: e.transpose(out=ptv2[:, 0:n], in_=G.on2.t[0:n, par, :],
                                                                                  identity=identb[0:n, 0:n]), [G.on2.r(par), RCW], [bkT2.r()])
                                DVE(lambda e, ptv2=ptv2, n=n, cs=cs: e.scalar_tensor_tensor(
                                    out=oT.t[:, 8 + h, cs:cs + n], in0=ptv2[:, 0:n], scalar=gdnn.t[:, 0:1], in1=srTb.t[:, G.j, cs:cs + n],
                                    op0=ALU.mult, op1=ALU.mult), [bkT2.r(), srTb.r(), RC], [oT.r(t)])
                            Cs.append(P.end_stream())
                        P.extend(Ls[0])
                        for i_ in range(1, len(Ls)):
                            P.extend(interleave(Ls[i_], Cs[i_ - 1]))
                        P.extend(Cs[-1])
                        seqs.append(P.end_stream())
                    k_ = len(seqs[0]) // (2 * max(ntl, 1))
                    P.extend(seqs[0][:k_])
                    P.extend(interleave(seqs[0][k_:], seqs[1]))

            cut(1)
            if WCH > 0:
                load_x_tiles(xw, WCH, True)
                cut(2)
                mixer_pass("warm")
                cut(6)
            else:
                DVE(lambda e: e.memset(Sgla.t, 0.0), [], Sgla.rs(range(4)))
                DVE(lambda e: e.memset(Sgdn.t, 0.0), [], Sgdn.rs(range(8)))
                DVE(lambda e: e.memset(tail.t, 0.0), [], [tail.r()])
                ACT(lambda e: e.activation(out=Sglab.t, in_=Sgla.t, func=AF.Copy), Sgla.rs(range(4)), Sglab.rs(range(4)))
                ACT(lambda e: e.activation(out=Sgdnb.t, in_=Sgdn.t, func=AF.Copy), Sgdn.rs(range(8)), Sgdnb.rs(range(8)))
            P.barrier()
            load_x_tiles(xm, NCH, False)
            cut(7)
            mod_fm(3, 2)
            mod_fm(4, 3)
            mod_scale(a2, n2fm, 2, 3)
            mixer_pass("main")
            cut(16)

            P.barrier()
            AR.off = persist_end
            x1 = sb("x1", [128, NT, D])
            growb = sb("grow", [128, 2, D])
            tmpx = sb("tmpx", [128, 2, 512])
            gb_off = AR.off
            gbuf = sb("gbuf", [128, T + 8])
            cacc = sb("cacc", [128, T + 8])
            actT = sb("actT", [128, GFF, T], BF16)
            if AR.off + D <= ARENA_WORDS:
                modrow = sb("modrow", [4, D])
            else:
                assert 2 * (T + 8) >= D
                modrow = Buf(arena_t[0:4, gb_off:gb_off + D], "modrow")
            xnB = Buf(tmpx.t.rearrange("p a b -> p (a b)").bitcast(BF16), "xnB")
            xnB._r = tmpx._r
            tring = Ring([0, 1])

            def mod_rows(sec_idx):
                P.dma("sp", stl.next(), [(modrow.t[3:4, :], bada_row[:, sec_idx * D:(sec_idx + 1) * D])], writes=[modrow.r()])
                for grp in range(8):
                    c0 = sec_idx * D + grp * 256
                    s = wload([(lambda t: wview(t, 16, 256), wsrc(w_ada, 0, 16, c0, c0 + 256))])
                    sv = wview(s.t, 16, 256)
                    bk = ringS.next()
                    for k in range(KD):
                        PE(lambda e, sv=sv, k=k, bk=bk: e.matmul(bk.t[0:3, 0:256], lhsT=siluT.t[:, k, :], rhs=sv[:, k, :],
                                                                 start=(k == 0), stop=(k == KD - 1)),
                           [s.r(), RCW], [bk.r()])
                    ACT(lambda e, bk=bk, grp=grp: e.copy(out=modrow.t[0:3, grp * 256:(grp + 1) * 256], in_=bk.t[0:3, 0:256]),
                        [bk.r(), modrow.r()], [modrow.r()])
                for v in range(2):
                    for cg in range(4):
                        bk = ringS.next()
                        PE(lambda e, bk=bk, v=v, cg=cg: e.matmul(
                            bk.t[:, :], lhsT=sel.t[:, v, :], rhs=modrow.t[0:4, cg * 512:(cg + 1) * 512],
                            start=True, stop=True), [modrow.r(), RCW, RC], [bk.r()])
                        ACT(lambda e, bk=bk, v=v, cg=cg: e.copy(out=growb.t[:, v, cg * 512:(cg + 1) * 512], in_=bk.t[:, :]),
                            [bk.r(), growb.r(v)], [growb.r(v)])

            def accumulate(bk, t, n, cg):
                gi = 0 if t == 0 else 1
                ti = tring.next()
                DVE(lambda e: e.tensor_tensor(
                    out=tmpx.t[0:n, ti, :], in0=bk.t[0:n, :], in1=growb.t[0:n, gi, cg * 512:(cg + 1) * 512], op=ALU.mult),
                    [bk.r(), growb.r(gi), tmpx.r(ti)], [tmpx.r(ti)])
                DVE(lambda e: e.tensor_tensor(
                    out=x1.t[0:n, t, cg * 512:(cg + 1) * 512], in0=x1.t[0:n, t, cg * 512:(cg + 1) * 512],
                    in1=tmpx.t[0:n, ti, :], op=ALU.add), [tmpx.r(ti), x1.r(t)], [x1.r(t)])

            ringM.items = [banks[2], banks[3], banks[4], banks[7]]
            mod_rows(2)
            for t in range(NT):
                n = tile_rows(t, NCH)
                P.dma("sp", xlanes[t % 2], [(x1.t[0:n, t, :], xm[t * 128:t * 128 + n, :])], writes=[x1.r(t)])
            for cg in range(4):
                sl = [wload([(lambda t: wview(t, 8, 512), wsrc(w_out, hf * 1024, 8, cg * 512, (cg + 1) * 512))])
                      for hf in range(2)]
                for t in range(NT):
                    n = tile_rows(t, NCH)
                    bk = ringM.next()
                    for k in range(KD):
                        s = sl[k // 8]
                        svv = wview(s.t, 8, 512)
                        PE(lambda e, bk=bk, k=k, t=t, n=n, svv=svv: e.matmul(
                            bk.t[0:n, :], lhsT=oT.t[:, k, t * 128:t * 128 + n], rhs=svv[:, k % 8, :], start=(k == 0),
                            stop=(k == KD - 1)), [s.r(), oT.r(t)], [bk.r()])
                    accumulate(bk, t, n, cg)
            mod_rows(5)
            for t in range(NT):
                n = tile_rows(t, NCH)
                norm_transpose(x1.t[0:n, t, :], n, a2, oT, t * 128, main_conds(t, n), [x1.r(t)], [oT.r(t)], xnB)
            cut(20)
            P.barrier()

            fsegs = [(0, 1, 0), (1, 1, 1), (2, NCH - 2, 2)]
            wf = ffn_col(NCH - 1) + 64 - 2
            ngroups = (NFF + GFF - 1) // GFF
            for g in range(ngroups):
                f0 = g * GFF
                nf = min(GFF, NFF - f0)
                for fp in range(0, nf, 2):
                    npair = min(2, nf - fp)
                    c0w = (f0 + fp) * 128
                    sg = wload([(lambda t, npair=npair: wview(t, 16, 256, 0, npair * 128), wsrc(w_up, 0, 16, c0w, c0w + npair * 128))])
                    sv_ = wload([(lambda t, npair=npair: wview(t, 16, 256, 0, npair * 128),
                                  wsrc(w_up, 0, 16, DFF + c0w, DFF + c0w + npair * 128))])
                    sgv = wview(sg.t, 16, 256)
                    svv = wview(sv_.t, 16, 256)
                    for j in range(npair):
                        fi = f0 + fp + j
                        lj = fp + j

                        def evac_gate(bk, c0_, n_):
                            for (c, ln, d0) in chunk_runs(c0_, n_, ffn_col):
                                ACT(lambda e, c=c, ln=ln, d0=d0: e.copy(out=gbuf.t[:, d0:d0 + ln], in_=bk.t[:, c - c0_:c - c0_ + ln]),
                                    [bk.r(), gbuf.r()], [gbuf.r()])

                        proj_fm(oT, sg, lambda k, j=j: sgv[:, k, j * 128:(j + 1) * 128], T, evac_gate, 128)
                        for (c_first, n_c, slot) in fsegs:
                            d0 = ffn_col(c_first) - 2
                            if slot == 2:
                                DVE(lambda e, d0=d0: e.memset(gbuf.t[:, d0:d0 + 2], 0.0), [gbuf.r()], [gbuf.r()])
                            else:
                                P.dma("sp", stl.next(), [(gbuf.t[:, d0:d0 + 2], sffn_d[slot, fi])], writes=[gbuf.r()])
                        DVE(lambda e, fi=fi: e.tensor_scalar(
                            out=cacc.t[:, 2:2 + wf], in0=gbuf.t[:, 2:2 + wf], scalar1=fcw.t[:, fi, 2:3],
                            scalar2=fcb.t[:, fi:fi + 1], op0=ALU.mult, op1=ALU.add), [gbuf.r(), RC, cacc.r()], [cacc.r()])
                        for i in range(2):
                            DVE(lambda e, fi=fi, i=i: e.scalar_tensor_tensor(
                                out=cacc.t[:, 2:2 + wf], in0=gbuf.t[:, i:i + wf], scalar=fcw.t[:, fi, i:i + 1],
                                in1=cacc.t[:, 2:2 + wf], op0=ALU.mult, op1=ALU.add), [gbuf.r(), cacc.r(), RC], [cacc.r()])
                        for (c_first, n_c, slot) in fsegs:
                            e0 = ffn_col(c_first + n_c - 1) + 64
                            P.dma("sp", olr.next(), [(offn_d[slot, fi], gbuf.t[:, e0 - 2:e0])], reads=[gbuf.r()])
                        ACT(lambda e: e.activation(out=cacc.t[:, 2:2 + wf], in_=cacc.t[:, 2:2 + wf], func=AF.Silu),
                            [cacc.r()], [cacc.r()])

                        def evac_val(bk, c0_, n_, lj=lj):
                            for (c, ln, d0) in chunk_runs(c0_, n_, ffn_col):
                                DVE(lambda e, c=c, ln=ln, d0=d0: e.tensor_tensor(
                                    out=actT.t[:, lj, c:c + ln], in0=bk.t[:, c - c0_:c - c0_ + ln], in1=cacc.t[:, d0:d0 + ln],
                                    op=ALU.mult), [bk.r(), cacc.r(), actT.r(lj)], [actT.r(lj)])

                        proj_fm(oT, sv_, lambda k, j=j: svv[:, k, j * 128:(j + 1) * 128], T, evac_val, 128)
                for cg in range(4):
                    sd = wload([(lambda t, nf=nf: wview(t, nf, 512), wsrc(w_down, f0 * 128, nf, cg * 512, (cg + 1) * 512))])
                    sdv = wview(sd.t, nf, 512)
                    for t in range(NT):
                        n = tile_rows(t, NCH)
                        bk = ringM.next()
                        for lj in range(nf):
                            PE(lambda e, bk=bk, lj=lj, t=t, n=n, sdv=sdv: e.matmul(
                                bk.t[0:n, :], lhsT=actT.t[:, lj, t * 128:t * 128 + n], rhs=sdv[:, lj, :], start=(lj == 0),
                                stop=(lj == nf - 1)), [sd.r(), actT.r(lj)], [bk.r()])
                        accumulate(bk, t, n, cg)
            cut(30)
            P.barrier()
            fnrow = growb.t[:, 0, :]
            P.dma("sp", lc, [(fnrow, fnorm_row.to_broadcast([128, D]))], writes=[growb.r(0)])
            ylanes = [P.lane("y%d" % i) for i in range(2)]
            for t in range(NT):
                n = tile_rows(t, NCH)
                ACT(lambda e, n=n, t=t: e.activation(out=growb.t[0:n, 1, :], in_=x1.t[0:n, t, :], func=AF.Square,
                                                     accum_out=stat.t[0:n, 6:7]), [x1.r(t)], [growb.r(1), stat.r()])
                rstd_from_ss(stat.t[0:n, 6:7], stat.t[0:n, 7:8], 1.0 / D, [stat.r()])
                DVE(lambda e, n=n, t=t: e.scalar_tensor_tensor(
                    out=x1.t[0:n, t, :], in0=x1.t[0:n, t, :], scalar=stat.t[0:n, 7:8], in1=fnrow[0:n, :], op0=ALU.mult,
                    op1=ALU.mult), [x1.r(t), stat.r(), growb.r(0)], [x1.r(t)])
                P.dma("sp", ylanes[t % 2], [(y_d[t * 128:t * 128 + n, :], x1.t[0:n, t, :])], reads=[x1.r(t)])
        except StopBuild:
            pass
        P.emit()
        build.info = dict(nops=P.nops, nticks=P.nticks)
    return nc


def fm(v, n):
    return np.ascontiguousarray(np.asarray(v, np.float32).reshape(n, 128).T)


def shared_inputs(inp, cfg):
    f = lambda a: np.ascontiguousarray(np.asarray(a, np.float32))
    nff = cfg.NFF
    return {
        "cst": host_consts(),
        "w_ada": f(inp["w_ada"][0]),
        "bada_fm": fm(inp["b_ada"][0], 96),
        "bada_row": f(inp["b_ada"][0]).reshape(1, -1),
        "norm1_fm": fm(inp["norm1"][0], 16),
        "norm2_fm": fm(inp["norm2"][0], 16),
        "fnorm_row": f(inp["final_norm"]).reshape(1, -1),
        "w_in": f(inp["w_in"][0]),
        "gla_wg": f(inp["gla_wg"][0]),
        "gla_bg_fm": fm(inp["gla_bg"][0], 4),
        "gla_norm_fm": fm(inp["gla_norm"][0], 2),
        "gdn_norm_fm": fm(inp["gdn_norm"][0], 1),
        "gdn_convw_fm": np.ascontiguousarray(f(inp["gdn_conv_w"][0]).reshape(4, 24, 128).transpose(2, 1, 0)),
        "alog_row": f(inp["gdn_a_log"][0]).reshape(1, 8),
        "dtb_row": f(inp["gdn_dt_bias"][0]).reshape(1, 8),
        "w_out": f(inp["w_out"][0]),
        "w_up": f(inp["w_up"][0]),
        "ffn_convw_fm": np.ascontiguousarray(f(inp["ffn_conv_w"][0]).reshape(3, nff, 128).transpose(2, 1, 0)),
        "ffn_convb_fm": fm(inp["ffn_conv_b"][0], nff),
        "w_down": f(inp["w_down"][0]),
    }


def core_inputs(cfg, xs2, cs2, xlb, xp, cp, xwarm, flagv, sgla2, sgdn2, sconv2, sffn2):
    f = lambda a: np.ascontiguousarray(np.asarray(a, np.float32))
    nff = cfg.NFF
    xm = np.concatenate([xs2[0], xs2[1], xlb, xp], axis=0)
    crow = np.stack([cs2[0], cs2[1], cp], axis=0)
    return {
        "xm": f(xm),
        "xw": f(xwarm) if cfg.WCH > 0 else np.zeros((64, D), np.float32),
        "cT": np.ascontiguousarray(f(crow).reshape(3, KD, 128).transpose(2, 1, 0)),
        "flag": np.full((128, 1), flagv, np.float32),
        "sgla": f(sgla2),
        "sgdn": f(sgdn2),
        "sconv": np.ascontiguousarray(f(sconv2).reshape(2, 3, 24, 128).transpose(0, 2, 3, 1)),
        "sffn": np.ascontiguousarray(f(sffn2).reshape(2, 2, nff, 128).transpose(0, 2, 3, 1)),
    }


_CACHE = {}


def kernel(**inp):
    cfg = Cfg()
    if "nc" not in _CACHE:
        _CACHE["nc"] = build(cfg)
    nc = _CACHE["nc"]
    xp_, xs_ = np.asarray(inp["x_prompt"]), np.asarray(inp["x_sample"])
    cp_, cs_ = np.asarray(inp["c_prompt"]), np.asarray(inp["c_sample"])
    sh = shared_inputs(inp, cfg)
    in_maps = []
    for c in range(8):
        b, h = c // 2, c % 2
        s0, s1 = 2 * c, 2 * c + 1
        lb0 = 1024 * h - 64 if h else 0
        m = core_inputs(cfg, [xs_[s0], xs_[s1]], [cs_[s0], cs_[s1]], xp_[b, lb0:lb0 + 64],
                        xp_[b, 1024 * h:1024 * h + 1024], cp_[b], xp_[b, 0:960], float(h),
                        np.asarray(inp["state_gla"])[0, s0:s1 + 1], np.asarray(inp["state_gdn"])[0, s0:s1 + 1],
                        np.asarray(inp["state_gdn_conv"])[0, s0:s1 + 1], np.asarray(inp["state_ffn_conv"])[0, s0:s1 + 1])
        m.update(sh)
        in_maps.append(m)
    res = run_bass_kernel_spmd(nc, in_maps, core_ids=list(range(8)))
    r = res.results
    B, S = xp_.shape[0], xp_.shape[1]
    nff = cfg.NFF
    y_p = np.zeros((B, S, D), np.float32)
    y_s = np.zeros(xs_.shape, np.float32)
    p_gla = np.zeros((1, B, 4, 128, 256), np.float32)
    p_gdn = np.zeros((1, B, 8, 128, 128), np.float32)
    p_conv = np.zeros((1, B, 3, 3072), np.float32)
    p_ffn = np.zeros((1, B, 2, 128 * nff), np.float32)
    s_gla = np.zeros((1, 16, 4, 128, 256), np.float32)
    s_gdn = np.zeros((1, 16, 8, 128, 128), np.float32)
    s_conv = np.zeros((1, 16, 3, 3072), np.float32)
    s_ffn = np.zeros((1, 16, 2, 128 * nff), np.float32)
    cv = lambda a: np.asarray(a).transpose(2, 0, 1).reshape(a.shape[2], -1)
    for c in range(8):
        b, h = c // 2, c % 2
        o = r[c]
        y = np.asarray(o["y"])
        y_s[2 * c] = y[0:64]
        y_s[2 * c + 1] = y[64:128]
        y_p[b, 1024 * h:1024 * h + 1024] = y[192:192 + 1024]
        for j in range(2):
            s_gla[0, 2 * c + j] = o["ogla"][j]
            s_gdn[0, 2 * c + j] = o["ogdn"][j]
            s_conv[0, 2 * c + j] = cv(o["oconv"][j])
            s_ffn[0, 2 * c + j] = cv(o["offn"][j])
        if h == 1:
            p_gla[0, b] = o["ogla"][2]
            p_gdn[0, b] = o["ogdn"][2]
            p_conv[0, b] = cv(o["oconv"][2])
            p_ffn[0, b] = cv(o["offn"][2])
    return (y_p, y_s, p_gla, p_gdn, p_conv, p_ffn, s_gla, s_gdn, s_conv, s_ffn)
```

```python
import numpy as np
from contextlib import ExitStack
import concourse.bass as bass
import concourse.mybir as mybir
from concourse.bass_utils import run_bass_kernel_spmd

F32 = mybir.dt.float32
BF16 = mybir.dt.bfloat16
AF = mybir.ActivationFunctionType
ALU = mybir.AluOpType

D = 2048
KD = 16
DIN = 7200
NMOD = 6
EPS = 1e-6
O_AQ, O_AK, O_AV, O_AR, O_LR = 0, 512, 1024, 2048, 3072
O_BQ, O_BK, O_BV, O_BG, O_BB = 3088, 4112, 5136, 6160, 7184
BIG = 30000.0


class Res:
    __slots__ = ("name", "w", "r")

    def __init__(self, name=""):
        self.name = name
        self.w = None
        self.r = {}


class Lane:
    def __init__(self, sem, name):
        self.sem = sem
        self.count = 0
        self.name = name


class Op:
    __slots__ = ("waits", "fn", "lane", "ninc")

    def __init__(self, waits, fn, lane, ninc=1):
        self.waits = waits
        self.fn = fn
        self.lane = lane
        self.ninc = ninc


class _Rec:
    def __init__(self):
        self.calls = []

    def __getattr__(self, name):
        def f(*a, **kw):
            self.calls.append((name, a, kw))
        return f


class Prog:
    ENG = ("pe", "act", "dve", "pool", "sp")

    def __init__(self, nc, stack):
        self.nc = nc
        self.stack = stack
        self.ops = {e: [] for e in self.ENG}
        self.seen = {e: {} for e in self.ENG}
        self.sem = {}
        for e in self.ENG:
            self.sem[e] = stack.enter_context(nc.semaphore("s_" + e))
        self.lanes = []
        self.main = []
        self.cur = self.main
        self._stack = []

    def lane(self, name):
        sem = self.stack.enter_context(self.nc.semaphore("l_" + name))
        ln = Lane(sem, name)
        self.lanes.append(ln)
        return ln

    def _collect(self, eng, reads, writes, skip_lane=None):
        need = []
        for r in reads:
            if r.w is not None:
                need.append(r.w)
        for w in writes:
            if w.w is not None:
                need.append(w.w)
            need.extend(w.r.values())
        out = {}
        seen = self.seen[eng]
        for ref in need:
            key = ref[1]
            if ref[0] == "E":
                if key == eng and eng in ("pe", "sp"):
                    continue
            elif key is skip_lane:
                continue
            val = ref[2]
            if seen.get(key, -1) >= val:
                continue
            if key not in out or out[key][2] < val:
                out[key] = ref
        for key, ref in out.items():
            seen[key] = ref[2]
        return list(out.values())

    def op(self, eng, fn, reads=(), writes=()):
        rec = _Rec()
        fn(rec)
        assert len(rec.calls) == 1
        name, a, kw = rec.calls[0]

        def fn2(e, name=name, a=a, kw=kw):
            return getattr(e, name)(*a, **kw)

        self.cur.append(("op", eng, fn2, list(reads), list(writes)))

    def dma(self, q, lane, pairs, reads=(), writes=()):
        self.cur.append(("dma", q, lane, list(pairs), list(reads), list(writes)))

    def barrier(self):
        self.cur.append(("bar",))

    def begin_stream(self):
        self._stack.append(self.cur)
        self.cur = []
        return self.cur

    def end_stream(self):
        s_ = self.cur
        self.cur = self._stack.pop()
        return s_

    def extend(self, items):
        self.cur.extend(items)

    def _schedule(self):
        for it in self.main:
            if it[0] == "op":
                self._op_now(it[1], it[2], it[3], it[4])
            elif it[0] == "dma":
                self._dma_now(it[1], it[2], it[3], it[4], it[5])
            else:
                self._barrier_now()

    def _op_now(self, eng, fn, reads=(), writes=()):
        if any(r.name.startswith("bank") for r in reads):
            writes = list(writes) + [r for r in reads if r.name.startswith("bank")]
            reads = [r for r in reads if not r.name.startswith("bank")]
        waits = self._collect(eng, reads, writes)
        idx = len(self.ops[eng])
        self.ops[eng].append(Op(waits, fn, None))
        ref = ("E", eng, idx)
        for r in reads:
            r.r[eng] = ref
        for w in writes:
            w.w = ref
            w.r = {}
        return ref

    def _dma_now(self, q, lane, pairs, reads=(), writes=()):
        waits = self._collect(q, reads, writes, skip_lane=lane)
        if lane.count and self.seen[q].get(lane, -1) < lane.count:
            waits.append(("L", lane, lane.count))
            self.seen[q][lane] = lane.count
        ref = None
        for i, (o, s) in enumerate(pairs):
            lane.count += 16
            ref = ("L", lane, lane.count)

            def fn(e, o=o, s=s):
                return e.dma_start(out=o, in_=s)

            self.ops[q].append(Op(waits if i == 0 else [], fn, lane))
        for r in reads:
            r.r[lane] = ref
        for w in writes:
            w.w = ref
            w.r = {}
        return ref

    def _barrier_now(self):
        last = {}
        for e in self.ENG:
            for idx in range(len(self.ops[e]) - 1, -1, -1):
                o = self.ops[e][idx]
                if o.lane is None and o.fn is not None:
                    last[e] = ("E", e, idx)
                    break
        lanes = [("L", ln, ln.count) for ln in self.lanes if ln.count]
        for e in self.ENG:
            waits = []
            seen = self.seen[e]
            for e2, ref in last.items():
                if e2 == e:
                    continue
                if seen.get(e2, -1) < ref[2]:
                    waits.append(ref)
                    seen[e2] = ref[2]
            for ref in lanes:
                if seen.get(ref[1], -1) < ref[2]:
                    waits.append(ref)
                    seen[ref[1]] = ref[2]
            if waits:
                self.ops[e].append(Op(waits, None, None))

    def emit(self):
        nc = self.nc
        self._schedule()
        sig = {e: set() for e in self.ENG}
        for e in self.ENG:
            for op in self.ops[e]:
                for ref in op.waits:
                    if ref[0] == "E":
                        sig[ref[1]].add(ref[2])
        tick = {e: {idx: k + 1 for k, idx in enumerate(sorted(sig[e]))} for e in self.ENG}
        final_waits = [(ln.sem, ln.count) for ln in self.lanes if ln.count]

        def run(e, eng):
            for idx, op in enumerate(self.ops[e]):
                for ref in op.waits:
                    if ref[0] == "E":
                        eng.wait_ge(self.sem[ref[1]], tick[ref[1]][ref[2]])
                    else:
                        eng.wait_ge(ref[1].sem, ref[2])
                if op.fn is None:
                    continue
                ins = op.fn(eng)
                if op.lane is not None:
                    ins.then_inc(op.lane.sem, 16)
                elif idx in tick[e]:
                    ins.then_inc(self.sem[e], 1)
            if e == "sp":
                for sem, val in final_waits:
                    eng.wait_ge(sem, val)

        with nc.Block() as block:
            block.tensor(lambda eng: run("pe", eng))
            block.scalar(lambda eng: run("act", eng))
            block.vector(lambda eng: run("dve", eng))
            block.gpsimd(lambda eng: run("pool", eng))
            block.sync(lambda eng: run("sp", eng))
        self.nticks = {e: len(tick[e]) for e in self.ENG}
        self.nops = {e: len(self.ops[e]) for e in self.ENG}


def interleave(a, b):
    out = []
    i = j = 0
    la, lb = len(a), len(b)
    while i < la or j < lb:
        if j >= lb or (i < la and i * lb <= j * la):
            out.append(a[i])
            i += 1
        else:
            out.append(b[j])
            j += 1
    return out


class Buf:
    def __init__(self, t, name):
        self.t = t
        self.name = name
        self._r = {}

    def r(self, key=0):
        if key not in self._r:
            self._r[key] = Res("%s.%s" % (self.name, key))
        return self._r[key]

    def rs(self, keys):
        return [self.r(k) for k in keys]


class Ring:
    def __init__(self, items):
        self.items = items
        self.i = 0

    def next(self):
        it = self.items[self.i % len(self.items)]
        self.i += 1
        return it


class Cfg:
    def __init__(self, npc=16, wch=15, dff=5632, gff=8):
        self.NPC = npc
        self.WCH = wch
        self.NCH = 3 + npc
        self.T = 64 * self.NCH
        self.NT = (self.NCH + 1) // 2
        self.TW = 64 * wch
        self.NTW = (wch + 1) // 2
        self.DFF = dff
        self.NFF = dff // 128
        self.GFF = gff


def host_consts():
    c = np.zeros((128, 8, 128), np.float32)
    i = np.arange(128)
    same = (i[:, None] // 64) == (i[None, :] // 64)
    c[:, 0, :] = np.eye(128)
    c[:, 1, :] = (same & (i[:, None] <= i[None, :]))
    c[:, 2, :] = same
    c[:, 3, :] = 1.0
    c[:, 4, :] = np.where(same & (i[None, :] < i[:, None]), 0.0, BIG)
    c[:, 5, :] = np.where(same & (i[:, None] <= i[None, :]), 0.0, BIG)
    c[0:64, 6, :] = 1.0
    c[64:128, 7, :] = 1.0
    return c


class Arena:
    def __init__(self, t, nwords):
        self.t = t
        self.n = nwords
        self.off = 0

    def take(self, name, shape, dt=F32):
        per = 1
        for s in shape[1:]:
            per *= s
        words = per if dt == F32 else (per + 1) // 2
        assert self.off + words <= self.n, "SBUF arena overflow at %s: %d + %d > %d" % (name, self.off, words, self.n)
        v = self.t[0:shape[0], self.off:self.off + words]
        self.off += words
        if dt != F32:
            v = v.bitcast(dt)[:, 0:per]
        if len(shape) == 3:
            v = v.rearrange("p (a b) -> p a b", a=shape[1])
        elif len(shape) == 4:
            v = v.rearrange("p (a b c) -> p a b c", a=shape[1], b=shape[2])
        return Buf(v, name)


ARENA_WORDS = 52200


class StopBuild(Exception):
    pass


STOP = [None]


def cut(k):
    if STOP[0] == k:
        raise StopBuild()


def build(cfg):
    nc = bass.Bass("TRN2", target_bir_lowering=False)
    T, NT, NCH, TW, NTW, WCH = cfg.T, cfg.NT, cfg.NCH, cfg.TW, cfg.NTW, cfg.WCH
    DFF, NFF, GFF = cfg.DFF, cfg.NFF, cfg.GFF
    TMX = max(T, TW)
    NTM = max(NT, NTW)

    def din(name, shape):
        return nc.dram_tensor(name, list(shape), F32, kind="ExternalInput").ap()

    def dout(name, shape):
        return nc.dram_tensor(name, list(shape), F32, kind="ExternalOutput").ap()

    xm = din("xm", [T, D])
    xw = din("xw", [max(TW, 64), D])
    cT_d = din("cT", [128, KD, 3])
    flag_d = din("flag", [128, 1])
    cst_d = din("cst", [128, 8, 128])
    sgla_d = din("sgla", [2, 4, 128, 256])
    sgdn_d = din("sgdn", [2, 8, 128, 128])
    sconv_d = din("sconv", [2, 24, 128, 3])
    sffn_d = din("sffn", [2, NFF, 128, 2])
    w_ada = din("w_ada", [D, NMOD * D])
    bada_fm = din("bada_fm", [128, 96])
    bada_row = din("bada_row", [1, NMOD * D])
    norm1_fm = din("norm1_fm", [128, KD])
    norm2_fm = din("norm2_fm", [128, KD])
    fnorm_row = din("fnorm_row", [1, D])
    w_in = din("w_in", [D, DIN])
    wg_d = din("gla_wg", [16, 512])
    nbg_fm = din("gla_bg_fm", [128, 4])
    glan_fm = din("gla_norm_fm", [128, 2])
    gdnn_fm = din("gdn_norm_fm", [128, 1])
    convw_fm = din("gdn_convw_fm", [128, 24, 4])
    alog_row = din("alog_row", [1, 8])
    dtb_row = din("dtb_row", [1, 8])
    w_out = din("w_out", [D, D])
    w_up = din("w_up", [D, 2 * DFF])
    fcw_fm = din("ffn_convw_fm", [128, NFF, 3])
    fcb_fm = din("ffn_convb_fm", [128, NFF])
    w_down = din("w_down", [DFF, D])

    y_d = dout("y", [T, D])
    ogla_d = dout("ogla", [3, 4, 128, 256])
    ogdn_d = dout("ogdn", [3, 8, 128, 128])
    oconv_d = dout("oconv", [3, 24, 128, 3])
    offn_d = dout("offn", [3, NFF, 128, 2])

    st = ExitStack()
    with st:
        P = Prog(nc, st)
        try:
            arena_t = st.enter_context(nc.sbuf_tensor("arena", [128, ARENA_WORDS], F32))
            AR = Arena(arena_t, ARENA_WORDS)
            sb = AR.take

            banks = [Buf(st.enter_context(nc.psum_tensor("bank%d" % i, [128, 512], F32)), "bank%d" % i)
                     for i in range(8)]
            ringT0 = Ring(banks[0:1])
            ringT1 = Ring(banks[1:2])
            ringM = Ring(banks[2:4])
            ringS = Ring(banks[5:7])
            ringC = Ring([banks[7], banks[1]])
            ringO = Ring([banks[4]])

            def ACT(fn, r=(), w=()):
                return P.op("act", fn, r, w)

            def DVE(fn, r=(), w=()):
                return P.op("dve", fn, r, w)

            def PE(fn, r=(), w=()):
                return P.op("pe", fn, r, w)

            cst = sb("cst", [128, 8, 128])
            cstb = sb("cstb", [128, 128], BF16)
            ident = cst.t[:, 0, :]
            U2 = cst.t[:, 1, :]
            B2 = cst.t[:, 2, :]
            ONES = cst.t[:, 3, :]
            BIGL = cst.t[:, 4, :]
            BIGU = cst.t[:, 5, :]
            SELA = cst.t[:, 6, :]
            SELB = cst.t[:, 7, :]
            identb = cstb.t
            RC = cst.r()
            oT = sb("oT", [128, KD, T], BF16)
            flag = sb("flag", [128, 1])
            siluT = sb("siluT", [128, KD, 3], BF16)
            cTs = sb("cTs", [128, KD, 3])
            modfm = sb("modfm", [128, 4, KD, 3])
            a1 = sb("a1", [128, 2, KD, 4])
            a2 = sb("a2", [128, 2, KD, 4])
            n1fm = sb("n1fm", [128, KD])
            n2fm = sb("n2fm", [128, KD])
            badafm = sb("badafm", [128, 96])
            sel = sb("sel", [4, 2, 128])
            wgs = sb("wgs", [16, 512])
            nbg = sb("nbg", [128, 4])
            glan = sb("glan", [128, 2])
            gdnn = sb("gdnn", [128, 1])
            convw = sb("convw", [128, 24, 4])
            fcw = sb("fcw", [128, NFF, 3])
            fcb = sb("fcb", [128, NFF])
            rowc = sb("rowc", [128, 2, 8])
            tail = sb("tail", [128, 24, 3])
            epsb = sb("epsb", [128, 1])
            stat = sb("stat", [128, 8])
            NSLOT = 3
            wsl = [sb("wsl%d" % i, [128, 4096], BF16) for i in range(NSLOT)]
            wlanes = [P.lane("w%d" % i) for i in range(NSLOT)]
            wring = Ring(list(range(NSLOT)))
            persist_end = AR.off

            def wload(parts):
                i = wring.next()
                s = wsl[i]
                pairs = [(dst(s.t), src) for (dst, src) in parts]
                P.dma("pool", wlanes[i], pairs, writes=[s.r()])
                return s

            def wview(t, nk, width, c0=0, c1=None):
                v = t[:, 0:nk * width].rearrange("p (k n) -> p k n", k=nk)
                return v[:, :, c0:(width if c1 is None else c1)]

            def wsrc(w, r0, nk, c0, c1):
                return w[r0:r0 + nk * 128, c0:c1].rearrange("(k p) n -> p k n", p=128)

            lc = P.lane("const")
            stl = Ring([P.lane("st%d" % i) for i in range(6)])
            olr = Ring([P.lane("out%d" % i) for i in range(6)])
            xlanes = [P.lane("x%d" % i) for i in range(2)]

            hT = sb("hT", [128, KD, TMX], BF16)
            xs_off = AR.off
            xs = sb("xs", [128, D])
            xnA = sb("xnA", [128, D], BF16)
            xs_end = AR.off
            TP = TMX + 16
            qraw = sb("qraw", [128, TP])
            kraw = sb("kraw", [128, TP])
            tmpA_off = AR.off
            tmpA = sb("tmpA", [128, TP])
            tmpB = sb("tmpB", [128, TMX])
            tmpB_end = AR.off
            tmpC_off = AR.off
            tmpC = sb("tmpC", [128, TMX])
            qTb = sb("qTb", [128, TMX], BF16)
            kTb = sb("kTb", [128, TMX], BF16)
            khTb = sb("khTb", [128, TMX], BF16)
            srTb = sb("srTb", [128, 2, T], BF16)
            vtb_off = AR.off
            vtb = sb("vtb", [128, NTM, 256], BF16)
            lrT = sb("lrT", [16, TMX])
            vtb_end = AR.off
            resetm = sb("resetm", [128, TMX], BF16)
            Sgla = sb("Sgla", [128, 4, 256])
            Sgdn = sb("Sgdn", [128, 8, 128])
            Sglab = sb("Sglab", [128, 4, 256], BF16)
            Sgdnb = sb("Sgdnb", [128, 8, 128], BF16)
            bag = sb("bag", [128, NTM, 16])
            gc = sb("gc", [128, 4, NTM, 8])
            dS = sb("dS", [128, 2, NTM, 8])
            eql = sb("eql", [128, TMX // 64])
            khtok = sb("khtok", [128, 2, 128], BF16)
            attn_sb = sb("attn_sb", [128, 2, 128], BF16)
            on_sb = sb("on_sb", [128, 2, 256], BF16)
            gdn0_off = AR.off
            kv_tok = sb("kv_tok", [128, 2, 2, 128], BF16)
            drv = sb("drv", [128, 2, 3, 128], BF16)
            gU = sb("gU", [128, 128])
            Dm = sb("Dm", [128, 2, 128])
            PQ = sb("PQ", [128, 2, 2, 128])
            Ym = sb("Ym", [128, 2, 128])
            TTb = sb("TTb", [128, 2, 128], BF16)
            nsk = sb("nsk", [128, 2, 128], BF16)
            erow = sb("erow", [128, 2, 128])
            at_sb = sb("at_sb", [128, 2, 128], BF16)
            qg_sb = sb("qg_sb", [128, 2, 128], BF16)
            u_sb = sb("u_sb", [128, 2, 128], BF16)
            on2 = sb("on2", [128, 2, 128], BF16)
            gdn0_end = AR.off
            scopeA_end = AR.off

            class _NS:
                pass

            GB = [_NS(), _NS()]
            for nm_ in ("kTb", "qTb", "khTb", "kv_tok", "drv", "gU", "Dm", "PQ", "Ym", "TTb", "nsk", "erow", "at_sb", "qg_sb",
                        "u_sb", "on2"):
                setattr(GB[0], nm_, locals()[nm_])
            AR.off = xs_off
            GB[1].kTb = sb("kTb1", [128, TMX], BF16)
            GB[1].qTb = sb("qTb1", [128, TMX], BF16)
            GB[1].khTb = sb("khTb1", [128, TMX], BF16)
            GB[1].PQ = sb("PQ1", [128, 2, 2, 128])
            GB[1].Ym = sb("Ym1", [128, 2, 128])
            GB[1].Dm = sb("Dm1", [128, 2, 128])
            assert AR.off <= xs_end, (AR.off, xs_end)
            yneed = 256 + 384 + 128 + 128 + 128 + 256 + 128 * 4
            if vtb_end - vtb_off >= yneed:
                AR.off = vtb_off
            else:
                AR.off = scopeA_end
            GB[1].kv_tok = sb("kv_tok1", [128, 2, 2, 128], BF16)
            GB[1].drv = sb("drv1", [128, 2, 3, 128], BF16)
            GB[1].gU = sb("gU1", [128, 128])
            GB[1].TTb = sb("TTb1", [128, 2, 128], BF16)
            GB[1].nsk = sb("nsk1", [128, 2, 128], BF16)
            GB[1].erow = sb("erow1", [128, 2, 128])
            GB[1].at_sb = sb("at_sb1", [128, 2, 128], BF16)
            GB[1].qg_sb = sb("qg_sb1", [128, 2, 128], BF16)
            GB[1].u_sb = sb("u_sb1", [128, 2, 128], BF16)
            GB[1].on2 = sb("on21", [128, 2, 128], BF16)
            if vtb_end - vtb_off >= yneed:
                assert AR.off <= vtb_end, (AR.off, vtb_end)
                AR.off = scopeA_end
            else:
                scopeA_end = AR.off
            if tmpB_end - tmpA_off >= D:
                xs2 = Buf(arena_t[:, tmpA_off:tmpA_off + D], "xs2")
            else:
                xs2 = xs
            xs_bufs = [xs, xs2]
            if TMX >= D // 2:
                xnA2 = Buf(arena_t[:, tmpC_off:tmpC_off + D // 2].bitcast(BF16), "xnA2")
            else:
                xnA2 = xnA
            xn_bufs = [xnA, xnA2]
            GA = [_NS(), _NS()]
            for nm_ in ("qTb", "kTb", "khTb", "srTb", "vtb", "khtok", "attn_sb", "on_sb", "eql"):
                setattr(GA[0], nm_, locals()[nm_])
            save_off = AR.off
            xneed = 3 * ((TMX + 1) // 2) + T
            zneed = NTM * 128 + 128 + 128 + 256 + TMX // 64 + 8
            if xs_end - xs_off >= xneed and gdn0_end - gdn0_off >= zneed:
                AR.off = xs_off
                GA[1].kTb = sb("kTb2", [128, TMX], BF16)
                GA[1].qTb = sb("qTb2", [128, TMX], BF16)
                GA[1].khTb = sb("khTb2", [128, TMX], BF16)
                GA[1].srTb = sb("srTb2", [128, 2, T], BF16)
                assert AR.off <= xs_end
                AR.off = gdn0_off
                GA[1].vtb = sb("vtb2", [128, NTM, 256], BF16)
                GA[1].khtok = sb("khtok2", [128, 2, 128], BF16)
                GA[1].attn_sb = sb("attn_sb2", [128, 2, 128], BF16)
                GA[1].on_sb = sb("on_sb2", [128, 2, 256], BF16)
                GA[1].eql = sb("eql2", [128, TMX // 64])
                assert AR.off <= gdn0_end
                AR.off = save_off
            else:
                GA[1].kTb = sb("kTb2", [128, TMX], BF16)
                GA[1].qTb = sb("qTb2", [128, TMX], BF16)
                GA[1].khTb = sb("khTb2", [128, TMX], BF16)
                GA[1].srTb = sb("srTb2", [128, 2, T], BF16)
                GA[1].vtb = sb("vtb2", [128, NTM, 256], BF16)
                GA[1].khtok = sb("khtok2", [128, 2, 128], BF16)
                GA[1].attn_sb = sb("attn_sb2", [128, 2, 128], BF16)
                GA[1].on_sb = sb("on_sb2", [128, 2, 256], BF16)
                GA[1].eql = sb("eql2", [128, TMX // 64])
                scopeA_end = AR.off
            GA[0].ringL, GA[0].ringC, GA[0].ringO, GA[0].sc = Ring([banks[5], banks[6]]), Ring([banks[7]]), Ring([banks[4]]), 2
            GA[1].ringL, GA[1].ringC, GA[1].ringO, GA[1].sc = Ring([banks[2], banks[3]]), Ring([banks[1]]), Ring([banks[0]]), 4
            GB[0].ringL, GB[0].ringC, GB[0].ringO, GB[0].j, GB[0].sc = Ring([banks[5], banks[6]]), Ring([banks[7]]), Ring([banks[4]]), 0, 4
            GB[1].ringL, GB[1].ringC, GB[1].ringO, GB[1].j, GB[1].sc = Ring([banks[2], banks[3]]), Ring([banks[1]]), Ring([banks[0]]), 1, 2

            cpairs = [
                (cst.t, cst_d), (flag.t, flag_d), (cTs.t, cT_d), (n1fm.t, norm1_fm), (n2fm.t, norm2_fm),
                (badafm.t, bada_fm), (wgs.t, wg_d), (nbg.t, nbg_fm), (glan.t, glan_fm), (gdnn.t, gdnn_fm),
                (convw.t, convw_fm), (fcw.t, fcw_fm), (fcb.t, fcb_fm),
                (rowc.t[:, 0, :], alog_row.to_broadcast([128, 8])), (rowc.t[:, 1, :], dtb_row.to_broadcast([128, 8])),
            ]
            P.dma("sp", lc, cpairs, writes=[RC])
            RCW = Res("constwork")
            DVE(lambda e: e.tensor_copy(out=cstb.t, in_=ident), [RC], [RCW])
            DVE(lambda e: e.memset(epsb.t, EPS), [], [RCW])
            DVE(lambda e: e.tensor_scalar(out=nbg.t, in0=nbg.t, scalar1=-1.0, scalar2=None, op0=ALU.mult), [RC, RCW], [RCW])
            DVE(lambda e: e.memset(sel.t, 0.0), [RCW], [RCW])
            DVE(lambda e: e.memset(resetm.t, 1.0), [RCW], [RCW])
            DVE(lambda e: e.memset(resetm.t.rearrange("p (c t) -> p c t", t=64)[:, :, 0:1], 0.0), [RCW], [RCW])
            P.dma("sp", P.lane("selc"), [
                (sel.t[0:1, 0, 0:64], cst_d[0:1, 3, 0:64]), (sel.t[1:2, 0, 64:128], cst_d[0:1, 3, 0:64]),
                (sel.t[3:4, 0, :], cst_d[0:1, 3, :]), (sel.t[2:3, 1, :], cst_d[0:1, 3, :]),
                (sel.t[3:4, 1, :], cst_d[0:1, 3, :])], reads=[], writes=[RCW])
            ACT(lambda e: e.activation(out=siluT.t, in_=cTs.t, func=AF.Silu), [RC, RCW], [RCW])
            ACT(lambda e: e.activation(out=rowc.t[:, 0, :], in_=rowc.t[:, 0, :], func=AF.Exp), [RC, RCW], [RCW])
            DVE(lambda e: e.tensor_scalar(out=rowc.t[:, 0, :], in0=rowc.t[:, 0, :], scalar1=-1.0, scalar2=None,
                                          op0=ALU.mult), [RCW], [RCW])

            def rstd_from_ss(ssv, outv, scale, res):
                ACT(lambda e: e.activation(out=outv, in_=ssv, func=AF.Ln, scale=scale, bias=epsb.t[0:ssv.shape[0], 0:1]),
                    res + [RCW], res)
                ACT(lambda e: e.activation(out=outv, in_=outv, func=AF.Exp, scale=-0.5), res, res)

            def tile_rows(t, nch):
                return 128 if 2 * t + 1 < nch else 64

            def tok_groups(tt):
                g = []
                c = 0
                while c < tt:
                    n = min(512, tt - c)
                    g.append((c, n))
                    c += n
                return g

            def norm_transpose(src_t, n, av, dstT, col0, conds, src_res, dst_res, xb):
                ss = stat.t[0:n, 0:1]
                rs = stat.t[0:n, 1:2]
                xres = [xb.r(0), xb.r(1)]
                ACT(lambda e: e.activation(out=xb.t[0:n, :], in_=src_t, func=AF.Square, accum_out=ss),
                    src_res, xres + [stat.r()])
                rstd_from_ss(ss, rs, 1.0 / D, [stat.r()])
                ACT(lambda e: e.activation(out=xb.t[0:n, :], in_=src_t, func=AF.Copy, scale=rs),
                    src_res + [stat.r()], xres)
                b0, b1 = banks[0], banks[1]
                pt = [b0.t[:].bitcast(BF16), b1.t[:].bitcast(BF16)]
                for k in range(KD):
                    bk = (b0, b1)[k // 8]
                    PE(lambda e, k=k: e.transpose(out=pt[k // 8][:, (k % 8) * 128:(k % 8) * 128 + n],
                                                  in_=xb.t[0:n, k * 128:(k + 1) * 128], identity=identb[0:n, 0:n]),
                       xres + [RCW], [bk.r()])
                for hb in range(2):
                    bk = (b0, b1)[hb]
                    pv = pt[hb].rearrange("p (k t) -> p k t", t=128)
                    for (c0, c1, cond) in conds:
                        w = c1 - c0
                        dv = dstT.t[:, hb * 8:(hb + 1) * 8, col0 + c0:col0 + c1]
                        DVE(lambda e, pv=pv, dv=dv, c0=c0, c1=c1, cond=cond, w=w, hb=hb: e.tensor_tensor(
                            out=dv, in0=pv[:, :, c0:c1],
                            in1=av.t[:, 0, hb * 8:(hb + 1) * 8, cond:cond + 1].to_broadcast([128, 8, w]), op=ALU.mult),
                            [bk.r(), av.r()], dst_res)
                        DVE(lambda e, dv=dv, cond=cond, w=w, hb=hb: e.tensor_tensor(
                            out=dv, in0=dv, in1=av.t[:, 1, hb * 8:(hb + 1) * 8, cond:cond + 1].to_broadcast([128, 8, w]),
                            op=ALU.add), [av.r()] + dst_res, dst_res)

            def main_conds(t, n):
                if t == 0:
                    return [(0, 64, 0), (64, 128, 1)]
                if t == 1:
                    return [(0, 64, 3), (64, 128, 2)]
                return [(0, n, 2)]

            def mod_fm(sec_idx, slot_idx):
                bk = ringS.next()
                for grp in range(8):
                    c0 = sec_idx * D + grp * 256
                    s = wload([(lambda t: wview(t, 16, 256), wsrc(w_ada, 0, 16, c0, c0 + 256))])
                    sv = wview(s.t, 16, 256)
                    for j in range(2):
                        ct = grp * 2 + j
                        for k in range(KD):
                            PE(lambda e, sv=sv, j=j, k=k, ct=ct: e.matmul(
                                bk.t[:, ct * 3:ct * 3 + 3], lhsT=sv[:, k, j * 128:(j + 1) * 128], rhs=siluT.t[:, k, :],
                                start=(k == 0), stop=(k == KD - 1)), [s.r(), RCW], [bk.r()])
                DVE(lambda e: e.tensor_tensor(
                    out=modfm.t[:, slot_idx, :, :], in0=bk.t[:, 0:48].rearrange("p (k c) -> p k c", c=3),
                    in1=badafm.t[:, sec_idx * 16:(sec_idx + 1) * 16].unsqueeze(2).to_broadcast([128, KD, 3]),
                    op=ALU.add), [bk.r(), RC], [modfm.r(slot_idx)])

            def mod_scale(av, nfm, sh_slot, sc_slot):
                DVE(lambda e: e.tensor_scalar(out=av.t[:, 0, :, 0:3], in0=modfm.t[:, sc_slot, :, :], scalar1=1.0,
                                              scalar2=None, op0=ALU.add), [modfm.r(sc_slot)], [av.r()])
                DVE(lambda e: e.tensor_tensor(out=av.t[:, 0, :, 0:3], in0=av.t[:, 0, :, 0:3],
                                              in1=nfm.t.unsqueeze(2).to_broadcast([128, KD, 3]), op=ALU.mult),
                    [av.r(), RC], [av.r()])
                DVE(lambda e: e.tensor_copy(out=av.t[:, 1, :, 0:3], in_=modfm.t[:, sh_slot, :, :]),
                    [modfm.r(sh_slot), av.r()], [av.r()])
                DVE(lambda e: e.tensor_scalar(out=av.t[:, :, :, 3:4], in0=av.t[:, :, :, 2:3], scalar1=flag.t[:, 0:1],
                                              scalar2=None, op0=ALU.mult), [av.r(), RC], [av.r()])

            cut(0)
            mod_fm(0, 0)
            mod_fm(1, 1)
            mod_scale(a1, n1fm, 0, 1)

            def load_x_tiles(src, nch, warm):
                ntl = (nch + 1) // 2
                for t in range(ntl):
                    n = tile_rows(t, nch)
                    xb_ = xs_bufs[t % 2]
                    P.dma("sp", xlanes[t % 2], [(xb_.t[0:n, :], src[t * 128:t * 128 + n, :])], writes=[xb_.r()])
                    conds = [(0, n, 3)] if warm else main_conds(t, n)
                    norm_transpose(xb_.t[0:n, :], n, a1, hT, t * 128, conds, [xb_.r()], [hT.r(t)], xn_bufs[t % 2])

            def chunk_layout(mode):
                if mode == "warm":
                    return WCH, [(0, WCH, None)]
                return NCH, [(0, 1, 0), (1, 1, 1), (2, NCH - 2, 2)]

            def conv_col(mode, ch):
                if mode == "warm":
                    return 3 + 64 * ch
                if ch == 0:
                    return 3
                if ch == 1:
                    return 70
                return 137 + 64 * (ch - 2)

            def ffn_col(ch):
                if ch == 0:
                    return 2
                if ch == 1:
                    return 68
                return 134 + 64 * (ch - 2)

            def chunk_runs(c0, n, colf):
                runs = []
                c = c0
                end = c0 + n
                while c < end:
                    ch = c // 64
                    ch_end = ch
                    while (ch_end + 1) * 64 < end and colf(ch_end + 1) == colf(ch_end) + 64:
                        ch_end += 1
                    ln = min(end, (ch_end + 1) * 64) - c
                    runs.append((c, ln, colf(ch) + (c - ch * 64)))
                    c += ln
                return runs

            def proj_fm(src, slot, sv_cols, tt, evac, width):
                for (c0, n) in tok_groups(tt):
                    bk = ringM.next()
                    tiles = list(range(c0 // 128, (c0 + n + 127) // 128))
                    for k in range(KD):
                        PE(lambda e, bk=bk, k=k, c0=c0, n=n: e.matmul(
                            bk.t[0:width, 0:n], lhsT=sv_cols(k), rhs=src.t[:, k, c0:c0 + n], start=(k == 0),
                            stop=(k == KD - 1)), [slot.r()] + src.rs(tiles), [bk.r()])
                    evac(bk, c0, n)

            def mixer_pass(mode):
                warm = mode == "warm"
                nch, segs = chunk_layout(mode)
                tt = 64 * nch
                ntl = (nch + 1) // 2

                torder = list(range(ntl)) if warm else list(range(1, ntl)) + [0]

                def seg_of(ch):
                    return [sg for sg in segs if sg[0] <= ch < sg[0] + sg[1]][0]

                s = wload([(lambda t: wview(t, 16, 32, 0, 16), wsrc(w_in, 0, 16, O_LR, O_LR + 16)),
                           (lambda t: wview(t, 16, 32, 16, 32), wsrc(w_in, 0, 16, O_BB, O_BB + 16))])
                sv = wview(s.t, 16, 32)
                proj_fm(hT, s, lambda k: sv[:, k, 0:16], tt,
                        lambda bk, c0, n: ACT(lambda e: e.copy(out=lrT.t[:, c0:c0 + n], in_=bk.t[0:16, 0:n]),
                                              [bk.r()], [lrT.r()]), 16)
                cut(31)
                if nch % 2 == 1:
                    DVE(lambda e: e.memset(bag.t[64:128, ntl - 1, :], 0.0), [bag.r()], [bag.r()])
                for t in range(ntl):
                    n = tile_rows(t, nch)
                    bk = ringS.next()
                    for k in range(KD):
                        PE(lambda e, bk=bk, k=k, t=t, n=n: e.matmul(bk.t[0:n, 0:16], lhsT=hT.t[:, k, t * 128:t * 128 + n],
                                                                    rhs=sv[:, k, 16:32], start=(k == 0), stop=(k == KD - 1)),
                           [s.r(), hT.r(t)], [bk.r()])
                    ACT(lambda e, bk=bk, t=t, n=n: e.copy(out=bag.t[0:n, t, :], in_=bk.t[0:n, 0:16]), [bk.r(), bag.r()], [bag.r()])
                cut(32)
                bv = bag.t[:, 0:ntl, :]
                ACT(lambda e: e.activation(out=bv[:, :, 0:8], in_=bv[:, :, 0:8], func=AF.Sigmoid), [bag.r()], [bag.r()])
                DVE(lambda e: e.tensor_tensor(out=bv[:, :, 8:16], in0=bv[:, :, 8:16],
                                              in1=rowc.t[:, 1, :].unsqueeze(1).to_broadcast([128, ntl, 8]), op=ALU.add),
                    [bag.r(), RC], [bag.r()])
                ACT(lambda e: e.activation(out=bv[:, :, 8:16], in_=bv[:, :, 8:16], func=AF.Exp), [bag.r()], [bag.r()])
                ACT(lambda e: e.activation(out=bv[:, :, 8:16], in_=bv[:, :, 8:16], func=AF.Ln, bias=1.0), [bag.r()], [bag.r()])
                DVE(lambda e: e.tensor_tensor(out=bv[:, :, 8:16], in0=bv[:, :, 8:16],
                                              in1=rowc.t[:, 0, :].unsqueeze(1).to_broadcast([128, ntl, 8]), op=ALU.mult),
                    [bag.r(), RCW], [bag.r()])
                cut(33)
                for t in range(ntl):
                    bk = ringS.next()
                    gv = bag.t[:, t, 8:16]
                    PE(lambda e, bk=bk, gv=gv: e.matmul(bk.t[:, 0:8], lhsT=U2, rhs=gv, start=True, stop=True),
                       [bag.r(), RC], [bk.r()])
                    PE(lambda e, bk=bk, gv=gv: e.matmul(bk.t[:, 8:16], lhsT=B2, rhs=gv, start=True, stop=True),
                       [bag.r(), RC], [bk.r()])
                    PE(lambda e, bk=bk, gv=gv: e.matmul(bk.t[:, 16:24], lhsT=SELA, rhs=gv, start=True, stop=True),
                       [bag.r(), RC], [bk.r()])
                    PE(lambda e, bk=bk, gv=gv: e.matmul(bk.t[:, 24:32], lhsT=SELB, rhs=gv, start=True, stop=True),
                       [bag.r(), RC], [bk.r()])
                    DVE(lambda e, bk=bk, t=t: e.tensor_copy(out=gc.t[:, 0:2, t, :],
                                                            in_=bk.t[:, 0:16].rearrange("p (a h) -> p a h", a=2)),
                        [bk.r(), gc.r()], [gc.r()])
                    ACT(lambda e, bk=bk, t=t: e.activation(out=dS.t[:, :, t, :],
                                                           in_=bk.t[:, 16:32].rearrange("p (a h) -> p a h", a=2), func=AF.Exp),
                        [bk.r(), dS.r()], [dS.r()])
                cut(34)
                g0, g1_, g2_, g3_ = (gc.t[:, i, 0:ntl, :] for i in range(4))
                DVE(lambda e: e.tensor_tensor(out=g2_, in0=g1_, in1=g0, op=ALU.subtract), [gc.r()], [gc.r()])
                ACT(lambda e: e.activation(out=g2_, in_=g2_, func=AF.Exp), [gc.r()], [gc.r()])
                ACT(lambda e: e.activation(out=g3_, in_=g0, func=AF.Exp), [gc.r()], [gc.r()])
                DVE(lambda e: e.tensor_tensor(out=g3_, in0=g3_, in1=bv[:, :, 0:8], op=ALU.mult), [gc.r(), bag.r()], [gc.r()])

                cut(3 if warm else 13)
                P.barrier()
                for hpair in range(0, 4, 2):
                    seqs = []
                    for j_ in range(2):
                        h = hpair + j_
                        A = GA[j_]
                        parts = [(lambda t: wview(t, 16, 256, 0, 128), wsrc(w_in, 0, 16, O_AK + h * 128, O_AK + (h + 1) * 128))]
                        if not warm:
                            parts.append((lambda t: wview(t, 16, 256, 128, 256),
                                          wsrc(w_in, 0, 16, O_AQ + h * 128, O_AQ + (h + 1) * 128)))
                        s1 = wload(parts)
                        s1v = wview(s1.t, 16, 256)
                        s2 = wload([(lambda t: wview(t, 16, 256), wsrc(w_in, 0, 16, O_AV + h * 256, O_AV + (h + 1) * 256))])
                        s2v = wview(s2.t, 16, 256)
                        for (c0, n) in tok_groups(tt):
                            bk = ringM.next()
                            PE(lambda e, bk=bk, c0=c0, n=n: e.matmul(bk.t[:, 0:n], lhsT=wgs.t[:, h * 128:(h + 1) * 128],
                                                                     rhs=lrT.t[:, c0:c0 + n], start=True, stop=True),
                               [lrT.r(), RC], [bk.r()])
                            ACT(lambda e, bk=bk, c0=c0, n=n: e.activation(out=tmpA.t[:, c0:c0 + n], in_=bk.t[:, 0:n], func=AF.Exp,
                                                                          scale=-1.0, bias=nbg.t[:, h:h + 1]),
                                [bk.r(), RCW], [tmpA.r()])
                        ACT(lambda e: e.activation(out=tmpA.t[:, 0:tt], in_=tmpA.t[:, 0:tt], func=AF.Ln, bias=1.0),
                            [tmpA.r()], [tmpA.r()])
                        DVE(lambda e: e.tensor_tensor_scan(out=tmpB.t[:, 0:tt], data0=resetm.t[:, 0:tt], data1=tmpA.t[:, 0:tt],
                                                           initial=0.0, op0=ALU.mult, op1=ALU.add),
                            [tmpA.r(), RCW], [tmpB.r()])
                        ACT(lambda e: e.activation(out=tmpA.t[:, 0:tt], in_=tmpB.t[:, 0:tt], func=AF.Exp, scale=-1.0 / 16),
                            [tmpB.r(), tmpA.r()], [tmpA.r()])
                        ACT(lambda e: e.activation(out=tmpC.t[:, 0:tt], in_=tmpB.t[:, 0:tt], func=AF.Exp, scale=1.0 / 16),
                            [tmpB.r()], [tmpC.r()])
                        eqv = A.eql.t[:, 0:nch]
                        DVE(lambda e: e.tensor_copy(out=eqv, in_=tmpA.t[:, 0:tt].rearrange("p (c t) -> p c t", t=64)[:, :, 63]),
                            [tmpA.r()], [A.eql.r()])
                        proj_fm(hT, s1, lambda k: s1v[:, k, 0:128], tt,
                                lambda bk, c0, n: ACT(lambda e: e.copy(out=kraw.t[:, c0:c0 + n], in_=bk.t[:, 0:n]),
                                                      [bk.r()], [kraw.r()]), 128)
                        if not warm:
                            proj_fm(hT, s1, lambda k: s1v[:, k, 128:256], tt,
                                    lambda bk, c0, n: ACT(lambda e: e.copy(out=qraw.t[:, c0:c0 + n], in_=bk.t[:, 0:n]),
                                                          [bk.r()], [qraw.r()]), 128)
                        for t in range(ntl):
                            n = tile_rows(t, nch)
                            bk = ringM.next()
                            for k in range(KD):
                                PE(lambda e, bk=bk, k=k, t=t, n=n: e.matmul(
                                    bk.t[0:n, 0:256], lhsT=hT.t[:, k, t * 128:t * 128 + n], rhs=s2v[:, k, :], start=(k == 0),
                                    stop=(k == KD - 1)), [s2.r(), hT.r(t)], [bk.r()])
                            ACT(lambda e, bk=bk, t=t, n=n: e.copy(out=A.vtb.t[0:n, t, :], in_=bk.t[0:n, 0:256]),
                                [bk.r()], [A.vtb.r(t)])
                        if not warm:
                            s3 = wload([(lambda t: wview(t, 16, 256), wsrc(w_in, 0, 16, O_AR + h * 256, O_AR + (h + 1) * 256))])
                            s3v = wview(s3.t, 16, 256)
                            for j in range(2):
                                proj_fm(hT, s3, lambda k, j=j: s3v[:, k, j * 128:(j + 1) * 128], tt,
                                        lambda bk, c0, n, j=j: ACT(lambda e: e.activation(
                                            out=A.srTb.t[:, j, c0:c0 + n], in_=bk.t[:, 0:n], func=AF.Silu), [bk.r()], [A.srTb.r()]), 128)
                        if not warm:
                            DVE(lambda e: e.scalar_tensor_tensor(out=A.qTb.t[:, 0:tt], in0=qraw.t[:, 0:tt], scalar=128.0 ** -0.5,
                                                                 in1=tmpA.t[:, 0:tt], op0=ALU.mult, op1=ALU.mult),
                                [qraw.r(), tmpA.r()], [A.qTb.r()])
                        DVE(lambda e: e.tensor_tensor(out=tmpC.t[:, 0:tt], in0=tmpC.t[:, 0:tt], in1=kraw.t[:, 0:tt], op=ALU.mult),
                            [kraw.r(), tmpC.r()], [tmpC.r()])
                        if not warm:
                            ACT(lambda e: e.copy(out=A.kTb.t[:, 0:tt], in_=tmpC.t[:, 0:tt]), [tmpC.r()], [A.kTb.r()])
                        DVE(lambda e: e.tensor_tensor(
                            out=A.khTb.t[:, 0:tt].rearrange("p (c t) -> p c t", t=64),
                            in0=tmpC.t[:, 0:tt].rearrange("p (c t) -> p c t", t=64),
                            in1=eqv.unsqueeze(2).to_broadcast([128, nch, 64]), op=ALU.mult), [tmpC.r(), A.eql.r()], [A.khTb.r()])
                        P.begin_stream()
                        Sv = Sgla.t[:, h, :]
                        Sb = Sglab.t[:, h, :]
                        RS = Sgla.r(h)
                        RSb = Sglab.r(h)
                        Ls, Cs = [], []
                        for it_, t in enumerate(torder):
                            n = tile_rows(t, nch)
                            cs = t * 128
                            par = it_ % 2
                            P.begin_stream()
                            rS, rT = A.ringL, A.ringL
                            bkT = rT.next()
                            ptv = bkT.t[:].bitcast(BF16)
                            PE(lambda e, ptv=ptv, cs=cs, n=n: e.transpose(out=ptv[0:n, 0:128], in_=A.khTb.t[:, cs:cs + n],
                                                                          identity=identb), [A.khTb.r(), RCW], [bkT.r()])
                            ACT(lambda e, ptv=ptv, n=n, par=par: e.copy(out=A.khtok.t[0:n, par, :], in_=ptv[0:n, 0:128]),
                                [bkT.r()], [A.khtok.r(par)])
                            if not warm:
                                bkA = rS.next()
                                PE(lambda e, bkA=bkA, cs=cs, n=n: e.matmul(bkA.t[0:n, 0:n], lhsT=A.kTb.t[:, cs:cs + n],
                                                                           rhs=A.qTb.t[:, cs:cs + n], start=True, stop=True),
                                   [A.kTb.r(), A.qTb.r()], [bkA.r()])
                                DVE(lambda e, bkA=bkA, n=n, par=par: e.scalar_tensor_tensor(
                                    out=A.attn_sb.t[0:n, par, 0:n], in0=BIGU[0:n, 0:n], scalar=0.0, in1=bkA.t[0:n, 0:n],
                                    op0=ALU.is_equal, op1=ALU.mult), [bkA.r(), RC], [A.attn_sb.r(par)])
                            Ls.append(P.end_stream())
                            P.begin_stream()
                            rS, rT = A.ringC, A.ringC
                            if not warm:
                                bkO = A.ringO.next()
                            for cc in range(n // 64):
                                ch = 2 * t + cc
                                p0 = cc * 64
                                seg = seg_of(ch)
                                if ch == seg[0]:
                                    if warm:
                                        DVE(lambda e: e.memset(Sv, 0.0), [RS], [RS])
                                        ACT(lambda e: e.activation(out=Sb, in_=Sv, func=AF.Copy), [RS], [RSb])
                                    elif seg[2] != 2:
                                        P.dma("sp", stl.next(), [(Sv, sgla_d[seg[2], h])], writes=[RS])
                                        ACT(lambda e: e.activation(out=Sb, in_=Sv, func=AF.Copy), [RS], [RSb])
                                if not warm:
                                    PE(lambda e, p0=p0, par=par, t=t, bkO=bkO: e.matmul(
                                        bkO.t[p0:p0 + 64, 0:256], lhsT=A.attn_sb.t[p0:p0 + 64, par, p0:p0 + 64],
                                        rhs=A.vtb.t[p0:p0 + 64, t, :], start=True, stop=False),
                                       [A.attn_sb.r(par), A.vtb.r(t)], [bkO.r()])
                                    PE(lambda e, p0=p0, cs=cs, bkO=bkO: e.matmul(
                                        bkO.t[p0:p0 + 64, 0:256], lhsT=A.qTb.t[:, cs + p0:cs + p0 + 64], rhs=Sb, start=False,
                                        stop=True), [A.qTb.r(), RSb], [bkO.r()])
                                bkS = rS.next()
                                PE(lambda e, bkS=bkS, p0=p0, par=par, t=t: e.matmul(
                                    bkS.t[:, 0:256], lhsT=A.khtok.t[p0:p0 + 64, par, :], rhs=A.vtb.t[p0:p0 + 64, t, :], start=True,
                                    stop=True), [A.khtok.r(par), A.vtb.r(t)], [bkS.r()])
                                DVE(lambda e, bkS=bkS, ch=ch: e.scalar_tensor_tensor(
                                    out=Sv, in0=Sv, scalar=A.eql.t[:, ch:ch + 1], in1=bkS.t[:, 0:256], op0=ALU.mult,
                                    op1=ALU.add), [RS, bkS.r(), A.eql.r()], [RS])
                                ACT(lambda e: e.activation(out=Sb, in_=Sv, func=AF.Copy), [RS], [RSb])
                                if (not warm) and ch == seg[0] + seg[1] - 1:
                                    P.dma("sp", olr.next(), [(ogla_d[seg[2], h], Sv)], reads=[RS])
                            if not warm:
                                ACT(lambda e, n=n, par=par, bkO=bkO: e.activation(
                                    out=A.on_sb.t[0:n, par, :], in_=bkO.t[0:n, 0:256], func=AF.Square,
                                    accum_out=stat.t[0:n, A.sc:A.sc + 1]), [bkO.r()], [A.on_sb.r(par), stat.r()])
                                rstd_from_ss(stat.t[0:n, A.sc:A.sc + 1], stat.t[0:n, A.sc + 1:A.sc + 2], 1.0 / 256, [stat.r()])
                                ACT(lambda e, n=n, par=par, bkO=bkO: e.activation(out=A.on_sb.t[0:n, par, :], in_=bkO.t[0:n, 0:256],
                                                                                  func=AF.Copy, scale=stat.t[0:n, A.sc + 1:A.sc + 2]),
                                    [bkO.r(), stat.r()], [A.on_sb.r(par)])
                                bkT2 = rT.next()
                                ptv2 = bkT2.t[:].bitcast(BF16)
                                for j in range(2):
                                    PE(lambda e, ptv2=ptv2, j=j, n=n, par=par: e.transpose(
                                        out=ptv2[:, j * 128:j * 128 + n], in_=A.on_sb.t[0:n, par, j * 128:(j + 1) * 128],
                                        identity=identb[0:n, 0:n]), [A.on_sb.r(par), RCW], [bkT2.r()])
                                for j in range(2):
                                    DVE(lambda e, ptv2=ptv2, j=j, n=n, cs=cs: e.scalar_tensor_tensor(
                                        out=oT.t[:, h * 2 + j, cs:cs + n], in0=ptv2[:, j * 128:j * 128 + n],
                                        scalar=glan.t[:, j:j + 1], in1=A.srTb.t[:, j, cs:cs + n], op0=ALU.mult, op1=ALU.mult),
                                        [bkT2.r(), A.srTb.r(), RC], [oT.r(t)])
                            Cs.append(P.end_stream())
                        P.extend(Ls[0])
                        for i_ in range(1, len(Ls)):
                            P.extend(interleave(Ls[i_], Cs[i_ - 1]))
                        P.extend(Cs[-1])
                        seqs.append(P.end_stream())
                    P.extend(interleave(seqs[0], seqs[1]))

                cut(4 if warm else 14)
                P.barrier()
                ncols = conv_col(mode, nch - 1) + 64
                colf = lambda ch: conv_col(mode, ch)
                for hpair in range(0, 8, 2):
                    seqs = []
                    for j_ in range(2):
                        h = hpair + j_
                        G = GB[j_]
                        s1 = wload([(lambda t: wview(t, 16, 256, 0, 128), wsrc(w_in, 0, 16, O_BK + h * 128, O_BK + (h + 1) * 128)),
                                    (lambda t: wview(t, 16, 256, 128, 256), wsrc(w_in, 0, 16, O_BV + h * 128, O_BV + (h + 1) * 128))])
                        s1v = wview(s1.t, 16, 256)
                        if warm:
                            s2 = wload([(lambda t: wview(t, 16, 256, 0, 128), wsrc(w_in, 0, 16, O_BQ + h * 128, O_BQ + (h + 1) * 128))])
                        else:
                            s2 = wload([(lambda t: wview(t, 16, 256, 0, 128), wsrc(w_in, 0, 16, O_BQ + h * 128, O_BQ + (h + 1) * 128)),
                                        (lambda t: wview(t, 16, 256, 128, 256), wsrc(w_in, 0, 16, O_BG + h * 128, O_BG + (h + 1) * 128))])
                        s2v = wview(s2.t, 16, 256)

                        def evac_conv(raw):
                            def f(bk, c0, n):
                                for (c, ln, d0) in chunk_runs(c0, n, colf):
                                    ACT(lambda e, c=c, ln=ln, d0=d0: e.copy(out=raw.t[:, d0:d0 + ln], in_=bk.t[:, c - c0:c - c0 + ln]),
                                        [bk.r()], [raw.r()])
                            return f

                        def conv_silu(raw, ci, dst):
                            for (c_first, n_c, slot) in segs:
                                d0 = colf(c_first) - 3
                                if warm:
                                    DVE(lambda e, d0=d0: e.memset(raw.t[:, d0:d0 + 3], 0.0), [raw.r()], [raw.r()])
                                elif slot == 2:
                                    DVE(lambda e, d0=d0: e.tensor_copy(out=raw.t[:, d0:d0 + 3], in_=tail.t[:, ci, :]),
                                        [raw.r(), tail.r()], [raw.r()])
                                else:
                                    P.dma("sp", stl.next(), [(raw.t[:, d0:d0 + 3], sconv_d[slot, ci])], writes=[raw.r()])
                            w = ncols - 3
                            DVE(lambda e: e.tensor_scalar(out=dst.t[:, 3:3 + w], in0=raw.t[:, 3:3 + w],
                                                          scalar1=convw.t[:, ci, 3:4], scalar2=None, op0=ALU.mult),
                                [raw.r(), RC, dst.r()], [dst.r()])
                            for i in range(3):
                                DVE(lambda e, i=i: e.scalar_tensor_tensor(
                                    out=dst.t[:, 3:3 + w], in0=raw.t[:, i:i + w], scalar=convw.t[:, ci, i:i + 1],
                                    in1=dst.t[:, 3:3 + w], op0=ALU.mult, op1=ALU.add), [raw.r(), dst.r(), RC], [dst.r()])
                            for (c_first, n_c, slot) in segs:
                                e0 = colf(c_first + n_c - 1) + 64
                                if warm:
                                    DVE(lambda e, e0=e0: e.tensor_copy(out=tail.t[:, ci, :], in_=raw.t[:, e0 - 3:e0]),
                                        [raw.r(), tail.r()], [tail.r()])
                                else:
                                    P.dma("sp", olr.next(), [(oconv_d[slot, ci], raw.t[:, e0 - 3:e0])], reads=[raw.r()])

                        def compact_silu(srcb, dst):
                            for (c_first, n_c, slot) in segs:
                                d0 = colf(c_first)
                                ACT(lambda e, d0=d0, c_first=c_first, n_c=n_c: e.activation(
                                    out=dst.t[:, c_first * 64:(c_first + n_c) * 64], in_=srcb.t[:, d0:d0 + n_c * 64],
                                    func=AF.Silu), [srcb.r(), dst.r()], [dst.r()])

                        def l2n(src, dstb, scale):
                            DVE(lambda e: e.tensor_tensor(out=tmpC.t[:, 0:tt], in0=src.t[:, 0:tt], in1=src.t[:, 0:tt], op=ALU.mult),
                                [src.r(), tmpC.r()], [tmpC.r()])
                            for (c0, n) in tok_groups(tt):
                                bk = ringM.next()
                                PE(lambda e, bk=bk, c0=c0, n=n: e.matmul(bk.t[:, 0:n], lhsT=ONES, rhs=tmpC.t[:, c0:c0 + n],
                                                                         start=True, stop=True), [tmpC.r(), RC], [bk.r()])
                                ACT(lambda e, bk=bk, c0=c0, n=n: e.activation(out=tmpC.t[:, c0:c0 + n], in_=bk.t[:, 0:n],
                                                                              func=AF.Ln, bias=epsb.t[:, 0:1], scale=1.0),
                                    [bk.r(), tmpC.r(), RCW], [tmpC.r()])
                            ACT(lambda e: e.activation(out=tmpC.t[:, 0:tt], in_=tmpC.t[:, 0:tt], func=AF.Exp, scale=-0.5),
                                [tmpC.r()], [tmpC.r()])
                            DVE(lambda e: e.scalar_tensor_tensor(out=dstb.t[:, 0:tt], in0=src.t[:, 0:tt], scalar=scale,
                                                                 in1=tmpC.t[:, 0:tt], op0=ALU.mult, op1=ALU.mult),
                                [src.r(), tmpC.r()], [dstb.r()])

                        proj_fm(hT, s1, lambda k: s1v[:, k, 0:128], tt, evac_conv(kraw), 128)
                        proj_fm(hT, s1, lambda k: s1v[:, k, 128:256], tt, evac_conv(qraw), 128)
                        conv_silu(kraw, 8 + h, tmpA)
                        compact_silu(tmpA, tmpB)
                        l2n(tmpB, G.kTb, 1.0)
                        cut(41)
                        conv_silu(qraw, 16 + h, tmpA)
                        compact_silu(tmpA, G.khTb)
                        cut(42)
                        if not warm:
                            proj_fm(hT, s2, lambda k: s2v[:, k, 0:128], tt, evac_conv(kraw), 128)
                            proj_fm(hT, s2, lambda k: s2v[:, k, 128:256], tt,
                                    lambda bk, c0, n: ACT(lambda e: e.activation(out=srTb.t[:, G.j, c0:c0 + n], in_=bk.t[:, 0:n],
                                                                                 func=AF.Silu), [bk.r()], [srTb.r()]), 128)
                            conv_silu(kraw, h, tmpA)
                            compact_silu(tmpA, tmpB)
                            l2n(tmpB, G.qTb, 128.0 ** -0.5)
                        else:
                            lt = ntl - 1
                            n = tile_rows(lt, nch)
                            bk = ringM.next()
                            for k in range(KD):
                                PE(lambda e, bk=bk, k=k, lt=lt, n=n: e.matmul(bk.t[:, 0:n], lhsT=s2v[:, k, 0:128],
                                                                              rhs=hT.t[:, k, lt * 128:lt * 128 + n], start=(k == 0),
                                                                              stop=(k == KD - 1)), [s2.r(), hT.r(lt)], [bk.r()])
                            ACT(lambda e, bk=bk, n=n: e.copy(out=tail.t[:, h, :], in_=bk.t[:, n - 3:n]), [bk.r(), tail.r()], [tail.r()])

                        cut(43)
                        P.begin_stream()
                        Sv = Sgdn.t[:, h, :]
                        Sb = Sgdnb.t[:, h, :]
                        RS = Sgdn.r(h)
                        RSb = Sgdnb.r(h)
                        vTb = G.khTb
                        Ls, Cs = [], []
                        for it_, t in enumerate(torder):
                            n = tile_rows(t, nch)
                            cs = t * 128
                            par = it_ % 2
                            P.begin_stream()
                            rS, rT = G.ringL, G.ringL
                            bkT = rT.next()
                            ptv = bkT.t[:].bitcast(BF16)
                            PE(lambda e, ptv=ptv, cs=cs, n=n: e.transpose(out=ptv[0:n, 0:128], in_=G.kTb.t[:, cs:cs + n], identity=identb),
                               [G.kTb.r(), RCW], [bkT.r()])
                            PE(lambda e, ptv=ptv, cs=cs, n=n: e.transpose(out=ptv[0:n, 128:256], in_=vTb.t[:, cs:cs + n], identity=identb),
                               [vTb.r(), RCW], [bkT.r()])
                            ACT(lambda e, ptv=ptv, n=n, par=par: e.copy(out=G.kv_tok.t[0:n, par, :, :].rearrange("p a d -> p (a d)"),
                                                                        in_=ptv[0:n, 0:256]), [bkT.r()], [G.kv_tok.r(par)])
                            DVE(lambda e, n=n, par=par, t=t: e.tensor_scalar(out=G.drv.t[0:n, par, 0, :], in0=G.kv_tok.t[0:n, par, 0, :],
                                                                             scalar1=gc.t[0:n, 2, t, h:h + 1], scalar2=None, op0=ALU.mult),
                                [G.kv_tok.r(par), gc.r()], [G.drv.r(par)])
                            ACT(lambda e, n=n, par=par, t=t: e.activation(out=G.drv.t[0:n, par, 1, :], in_=G.kv_tok.t[0:n, par, 1, :],
                                                                          func=AF.Copy, scale=bag.t[0:n, t, h:h + 1]),
                                [G.kv_tok.r(par), bag.r(), G.drv.r(par)], [G.drv.r(par)])
                            DVE(lambda e, n=n, par=par, t=t: e.tensor_scalar(out=G.drv.t[0:n, par, 2, :], in0=G.kv_tok.t[0:n, par, 0, :],
                                                                             scalar1=gc.t[0:n, 3, t, h:h + 1], scalar2=None, op0=ALU.mult),
                                [G.kv_tok.r(par), gc.r(), G.drv.r(par)], [G.drv.r(par)])
                            cut(44)
                            ACT(lambda e, n=n, t=t: e.activation(out=G.gU.t[0:n, 0:n], in_=U2[0:n, 0:n], func=AF.Copy,
                                                                 scale=bag.t[0:n, t, 8 + h:9 + h]), [bag.r(), RC, G.gU.r()], [G.gU.r()])
                            bkG = rS.next()
                            PE(lambda e, bkG=bkG, n=n: e.matmul(bkG.t[:, 0:n], lhsT=ONES[0:n, :], rhs=G.gU.t[0:n, 0:n], start=True,
                                                                stop=True), [G.gU.r(), RC], [bkG.r()])
                            DVE(lambda e, bkG=bkG, n=n, t=t: e.scalar_tensor_tensor(
                                out=G.Dm.t[0:n, 0, 0:n], in0=bkG.t[0:n, 0:n], scalar=gc.t[0:n, 0, t, h:h + 1], in1=BIGL[0:n, 0:n],
                                op0=ALU.subtract, op1=ALU.max), [bkG.r(), gc.r(), RC, G.Dm.r()], [G.Dm.r()])
                            DVE(lambda e, bkG=bkG, n=n, t=t: e.tensor_scalar(
                                out=G.Dm.t[0:n, 1, 0:n], in0=bkG.t[0:n, 0:n], scalar1=gc.t[0:n, 0, t, h:h + 1], scalar2=-1.0,
                                op0=ALU.subtract, op1=ALU.mult), [bkG.r(), gc.r(), G.Dm.r()], [G.Dm.r()])
                            DVE(lambda e, n=n: e.tensor_tensor(out=G.Dm.t[0:n, 1, 0:n], in0=G.Dm.t[0:n, 1, 0:n], in1=BIGU[0:n, 0:n],
                                                               op=ALU.max), [G.Dm.r(), RC], [G.Dm.r()])
                            ACT(lambda e, n=n: e.activation(out=G.Dm.t[0:n, :, 0:n], in_=G.Dm.t[0:n, :, 0:n], func=AF.Exp, scale=-1.0),
                                [G.Dm.r()], [G.Dm.r()])
                            if not warm:
                                ACT(lambda e, bkG=bkG, n=n, par=par: e.activation(out=G.erow.t[:, par, 0:n], in_=bkG.t[:, 0:n], func=AF.Exp),
                                    [bkG.r(), G.erow.r(par)], [G.erow.r(par)])
                                DVE(lambda e, n=n, par=par, cs=cs: e.tensor_tensor(out=G.qg_sb.t[:, par, 0:n], in0=G.qTb.t[:, cs:cs + n],
                                                                                  in1=G.erow.t[:, par, 0:n], op=ALU.mult),
                                    [G.qTb.r(), G.erow.r(par)], [G.qg_sb.r(par)])
                            cut(45)
                            bkA = rS.next()
                            PE(lambda e, bkA=bkA, cs=cs, n=n: e.matmul(bkA.t[0:n, 0:n], lhsT=G.kTb.t[:, cs:cs + n], rhs=G.kTb.t[:, cs:cs + n],
                                                                       start=True, stop=True), [G.kTb.r()], [bkA.r()])
                            if not warm:
                                PE(lambda e, bkA=bkA, cs=cs, n=n: e.matmul(bkA.t[0:n, 128:128 + n], lhsT=G.kTb.t[:, cs:cs + n],
                                                                           rhs=G.qTb.t[:, cs:cs + n], start=True, stop=True),
                                   [G.kTb.r(), G.qTb.r()], [bkA.r()])
                            DVE(lambda e, bkA=bkA, n=n, t=t: e.scalar_tensor_tensor(
                                out=G.PQ.t[0:n, 0, 0, 0:n], in0=bkA.t[0:n, 0:n], scalar=bag.t[0:n, t, h:h + 1], in1=G.Dm.t[0:n, 0, 0:n],
                                op0=ALU.mult, op1=ALU.mult), [bkA.r(), bag.r(), G.Dm.r(), G.PQ.r()], [G.PQ.r()])
                            if not warm:
                                DVE(lambda e, bkA=bkA, n=n, par=par: e.tensor_tensor(out=G.at_sb.t[0:n, par, 0:n], in0=bkA.t[0:n, 128:128 + n],
                                                                                    in1=G.Dm.t[0:n, 1, 0:n], op=ALU.mult),
                                    [bkA.r(), G.Dm.r()], [G.at_sb.r(par)])
                            bkX = rS.next()
                            PE(lambda e, bkX=bkX, n=n: e.matmul(bkX.t[0:n, 0:n], lhsT=G.PQ.t[0:n, 0, 0, 0:n], rhs=ident[0:n, 0:n], start=True, stop=True),
                               [G.PQ.r(), RC], [bkX.r()])
                            ACT(lambda e, bkX=bkX, n=n: e.copy(out=G.PQ.t[0:n, 0, 1, 0:n], in_=bkX.t[0:n, 0:n]), [bkX.r(), G.PQ.r()], [G.PQ.r()])
                            DVE(lambda e, bkX=bkX, n=n: e.tensor_tensor(out=G.Ym.t[0:n, 0, 0:n], in0=ident[0:n, 0:n], in1=bkX.t[0:n, 0:n],
                                                                        op=ALU.subtract), [bkX.r(), RC, G.Ym.r()], [G.Ym.r()])
                            cut(46)
                            cur = 0
                            for it in range(5):
                                nxt = 1 - cur
                                Pc, Qc = G.PQ.t[0:n, cur, 0, 0:n], G.PQ.t[0:n, cur, 1, 0:n]
                                Pn, Qn = G.PQ.t[0:n, nxt, 0, 0:n], G.PQ.t[0:n, nxt, 1, 0:n]
                                bkP = rS.next()
                                PE(lambda e, bkP=bkP, Pc=Pc, Qc=Qc, n=n: e.matmul(bkP.t[0:n, 0:n], lhsT=Qc, rhs=Pc, start=True, stop=True),
                                   [G.PQ.r()], [bkP.r()])
                                if it < 4:
                                    PE(lambda e, bkP=bkP, Pc=Pc, Qc=Qc, n=n: e.matmul(bkP.t[0:n, 128:128 + n], lhsT=Pc, rhs=Qc, start=True,
                                                                                     stop=True), [G.PQ.r()], [bkP.r()])
                                if it < 4:
                                    ACT(lambda e, bkP=bkP, nxt=nxt, n=n: e.copy(
                                        out=G.PQ.t[0:n, nxt, :, 0:n], in_=bkP.t[0:n, 0:256].rearrange("p (a b) -> p a b", a=2)[:, :, 0:n]),
                                        [bkP.r(), G.PQ.r()], [G.PQ.r()])
                                else:
                                    ACT(lambda e, bkP=bkP, Pn=Pn, n=n: e.copy(out=Pn, in_=bkP.t[0:n, 0:n]), [bkP.r(), G.PQ.r()], [G.PQ.r()])
                                bkY = rS.next()
                                yc, yn = G.Ym.t[0:n, cur, 0:n], G.Ym.t[0:n, nxt, 0:n]
                                PE(lambda e, bkY=bkY, Pn=Pn, yc=yc, n=n: e.matmul(bkY.t[0:n, 0:n], lhsT=Pn, rhs=yc, start=True, stop=True),
                                   [G.PQ.r(), G.Ym.r()], [bkY.r()])
                                DVE(lambda e, bkY=bkY, yc=yc, yn=yn, n=n: e.tensor_tensor(out=yn, in0=bkY.t[0:n, 0:n], in1=yc, op=ALU.add),
                                    [bkY.r(), G.Ym.r()], [G.Ym.r()])
                                cur = nxt
                            ACT(lambda e, n=n, par=par, cur=cur: e.copy(out=G.TTb.t[0:n, par, 0:n], in_=G.Ym.t[0:n, cur, 0:n]),
                                [G.Ym.r()], [G.TTb.r(par)])
                            cut(47)
                            bkK = rS.next()
                            PE(lambda e, bkK=bkK, n=n, par=par: e.matmul(bkK.t[:, 0:n], lhsT=G.drv.t[0:n, par, 2, :], rhs=G.TTb.t[0:n, par, 0:n],
                                                                         start=True, stop=True), [G.drv.r(par), G.TTb.r(par)], [bkK.r()])
                            ACT(lambda e, bkK=bkK, n=n, par=par: e.activation(out=G.nsk.t[:, par, 0:n], in_=bkK.t[:, 0:n], func=AF.Copy,
                                                                              scale=-1.0), [bkK.r()], [G.nsk.r(par)])
                            Ls.append(P.end_stream())
                            P.begin_stream()
                            rS, rT = G.ringC, G.ringC
                            if not warm:
                                bkO = G.ringO.next()
                            for cc in range(n // 64):
                                ch = 2 * t + cc
                                p0 = cc * 64
                                seg = seg_of(ch)
                                if ch == seg[0]:
                                    if warm:
                                        DVE(lambda e: e.memset(Sv, 0.0), [RS], [RS])
                                        ACT(lambda e: e.activation(out=Sb, in_=Sv, func=AF.Copy), [RS], [RSb])
                                    elif seg[2] != 2:
                                        P.dma("sp", stl.next(), [(Sv, sgdn_d[seg[2], h])], writes=[RS])
                                        ACT(lambda e: e.activation(out=Sb, in_=Sv, func=AF.Copy), [RS], [RSb])
                                bkU = rS.next()
                                PE(lambda e, bkU=bkU, p0=p0, par=par: e.matmul(
                                    bkU.t[p0:p0 + 64, 0:128], lhsT=G.TTb.t[p0:p0 + 64, par, p0:p0 + 64], rhs=G.drv.t[p0:p0 + 64, par, 1, :],
                                    start=True, stop=False), [G.TTb.r(par), G.drv.r(par)], [bkU.r()])
                                PE(lambda e, bkU=bkU, p0=p0, par=par: e.matmul(
                                    bkU.t[p0:p0 + 64, 0:128], lhsT=G.nsk.t[:, par, p0:p0 + 64], rhs=Sb, start=False, stop=True),
                                   [G.nsk.r(par), RSb], [bkU.r()])
                                ACT(lambda e, bkU=bkU, p0=p0, par=par: e.copy(out=G.u_sb.t[p0:p0 + 64, par, :], in_=bkU.t[p0:p0 + 64, 0:128]),
                                    [bkU.r()], [G.u_sb.r((par, cc))])
                                if not warm:
                                    PE(lambda e, p0=p0, par=par, bkO=bkO: e.matmul(
                                        bkO.t[p0:p0 + 64, 0:128], lhsT=G.qg_sb.t[:, par, p0:p0 + 64], rhs=Sb, start=True, stop=False),
                                       [G.qg_sb.r(par), RSb], [bkO.r()])
                                    PE(lambda e, p0=p0, par=par, bkO=bkO: e.matmul(
                                        bkO.t[p0:p0 + 64, 0:128], lhsT=G.at_sb.t[p0:p0 + 64, par, p0:p0 + 64], rhs=G.u_sb.t[p0:p0 + 64, par, :],
                                        start=False, stop=True), [G.at_sb.r(par), G.u_sb.r((par, cc))], [bkO.r()])
                                bkS = rS.next()
                                PE(lambda e, bkS=bkS, p0=p0, par=par: e.matmul(
                                    bkS.t[:, 0:128], lhsT=G.drv.t[p0:p0 + 64, par, 0, :], rhs=G.u_sb.t[p0:p0 + 64, par, :], start=True, stop=True),
                                   [G.drv.r(par), G.u_sb.r((par, cc))], [bkS.r()])
                                DVE(lambda e, bkS=bkS, cc=cc, t=t: e.scalar_tensor_tensor(
                                    out=Sv, in0=Sv, scalar=dS.t[:, cc, t, h:h + 1], in1=bkS.t[:, 0:128], op0=ALU.mult, op1=ALU.add),
                                    [RS, bkS.r(), dS.r()], [RS])
                                ACT(lambda e: e.activation(out=Sb, in_=Sv, func=AF.Copy), [RS], [RSb])
                                if (not warm) and ch == seg[0] + seg[1] - 1:
                                    P.dma("sp", olr.next(), [(ogdn_d[seg[2], h], Sv)], reads=[RS])
                            if not warm:
                                ACT(lambda e, n=n, par=par, bkO=bkO: e.activation(out=G.on2.t[0:n, par, :], in_=bkO.t[0:n, 0:128], func=AF.Square,
                                                                                  accum_out=stat.t[0:n, G.sc:G.sc + 1]), [bkO.r()], [G.on2.r(par), stat.r()])
                                rstd_from_ss(stat.t[0:n, G.sc:G.sc + 1], stat.t[0:n, G.sc + 1:G.sc + 2], 1.0 / 128, [stat.r()])
                                ACT(lambda e, n=n, par=par, bkO=bkO: e.activation(out=G.on2.t[0:n, par, :], in_=bkO.t[0:n, 0:128], func=AF.Copy,
                                                                                  scale=stat.t[0:n, G.sc + 1:G.sc + 2]), [bkO.r(), stat.r()], [G.on2.r(par)])
                                bkT2 = rT.next()
                                ptv2 = bkT2.t[:].bitcast(BF16)
                                PE(lambda e, ptv2=ptv2, n=n, par=par: e.transpose(out=ptv2[:, 0:n], in_=G.on2.t[0:n, par, :],
                                                                                  identity=identb[0:n, 0:n]), [G.on2.r(par), RCW], [bkT2.r()])
                                DVE(lambda e, ptv2=ptv2, n=n, cs=cs: e.scalar_tensor_tensor(
                                    out=oT.t[:, 8 + h, cs:cs + n], in0=ptv2[:, 0:n], scalar=gdnn.t[:, 0:1], in1=srTb.t[:, G.j, cs:cs + n],
                                    op0=ALU.mult, op1=ALU.mult), [bkT2.r(), srTb.r(), RC], [oT.r(t)])
                            Cs.append(P.end_stream())
                        P.extend(Ls[0])
                        for i_ in range(1, len(Ls)):
                            P.extend(interleave(Ls[i_], Cs[i_ - 1]))
                        P.extend(Cs[-1])
                        seqs.append(P.end_stream())
                    P.extend(interleave(seqs[0], seqs[1]))

            cut(1)
            if WCH > 0:
                load_x_tiles(xw, WCH, True)
                cut(2)
                mixer_pass("warm")
                cut(6)
            else:
                DVE(lambda e: e.memset(Sgla.t, 0.0), [], Sgla.rs(range(4)))
                DVE(lambda e: e.memset(Sgdn.t, 0.0), [], Sgdn.rs(range(8)))
                DVE(lambda e: e.memset(tail.t, 0.0), [], [tail.r()])
                ACT(lambda e: e.activation(out=Sglab.t, in_=Sgla.t, func=AF.Copy), Sgla.rs(range(4)), Sglab.rs(range(4)))
                ACT(lambda e: e.activation(out=Sgdnb.t, in_=Sgdn.t, func=AF.Copy), Sgdn.rs(range(8)), Sgdnb.rs(range(8)))
            P.barrier()
            load_x_tiles(xm, NCH, False)
            cut(7)
            mod_fm(3, 2)
            mod_fm(4, 3)
            mod_scale(a2, n2fm, 2, 3)
            mixer_pass("main")
            cut(16)

            P.barrier()
            AR.off = persist_end
            x1 = sb("x1", [128, NT, D])
            growb = sb("grow", [128, 2, D])
            tmpx = sb("tmpx", [128, 2, 512])
            gb_off = AR.off
            gbuf = sb("gbuf", [128, T + 8])
            cacc = sb("cacc", [128, T + 8])
            actT = sb("actT", [128, GFF, T], BF16)
            if AR.off + D <= ARENA_WORDS:
                modrow = sb("modrow", [4, D])
            else:
                assert 2 * (T + 8) >= D
                modrow = Buf(arena_t[0:4, gb_off:gb_off + D], "modrow")
            xnB = Buf(tmpx.t.rearrange("p a b -> p (a b)").bitcast(BF16), "xnB")
            xnB._r = tmpx._r
            tring = Ring([0, 1])

            def mod_rows(sec_idx):
                P.dma("sp", stl.next(), [(modrow.t[3:4, :], bada_row[:, sec_idx * D:(sec_idx + 1) * D])], writes=[modrow.r()])
                for grp in range(8):
                    c0 = sec_idx * D + grp * 256
                    s = wload([(lambda t: wview(t, 16, 256), wsrc(w_ada, 0, 16, c0, c0 + 256))])
                    sv = wview(s.t, 16, 256)
                    bk = ringS.next()
                    for k in range(KD):
                        PE(lambda e, sv=sv, k=k, bk=bk: e.matmul(bk.t[0:3, 0:256], lhsT=siluT.t[:, k, :], rhs=sv[:, k, :],
                                                                 start=(k == 0), stop=(k == KD - 1)),
                           [s.r(), RCW], [bk.r()])
                    ACT(lambda e, bk=bk, grp=grp: e.copy(out=modrow.t[0:3, grp * 256:(grp + 1) * 256], in_=bk.t[0:3, 0:256]),
                        [bk.r(), modrow.r()], [modrow.r()])
                for v in range(2):
                    for cg in range(4):
                        bk = ringS.next()
                        PE(lambda e, bk=bk, v=v, cg=cg: e.matmul(
                            bk.t[:, :], lhsT=sel.t[:, v, :], rhs=modrow.t[0:4, cg * 512:(cg + 1) * 512],
                            start=True, stop=True), [modrow.r(), RCW, RC], [bk.r()])
                        ACT(lambda e, bk=bk, v=v, cg=cg: e.copy(out=growb.t[:, v, cg * 512:(cg + 1) * 512], in_=bk.t[:, :]),
                            [bk.r(), growb.r(v)], [growb.r(v)])

            def accumulate(bk, t, n, cg):
                gi = 0 if t == 0 else 1
                ti = tring.next()
                DVE(lambda e: e.tensor_tensor(
                    out=tmpx.t[0:n, ti, :], in0=bk.t[0:n, :], in1=growb.t[0:n, gi, cg * 512:(cg + 1) * 512], op=ALU.mult),
                    [bk.r(), growb.r(gi), tmpx.r(ti)], [tmpx.r(ti)])
                DVE(lambda e: e.tensor_tensor(
                    out=x1.t[0:n, t, cg * 512:(cg + 1) * 512], in0=x1.t[0:n, t, cg * 512:(cg + 1) * 512],
                    in1=tmpx.t[0:n, ti, :], op=ALU.add), [tmpx.r(ti), x1.r(t)], [x1.r(t)])

            ringM.items = [banks[2], banks[3], banks[4], banks[7]]
            mod_rows(2)
            for t in range(NT):
                n = tile_rows(t, NCH)
                P.dma("sp", xlanes[t % 2], [(x1.t[0:n, t, :], xm[t * 128:t * 128 + n, :])], writes=[x1.r(t)])
            for cg in range(4):
                sl = [wload([(lambda t: wview(t, 8, 512), wsrc(w_out, hf * 1024, 8, cg * 512, (cg + 1) * 512))])
                      for hf in range(2)]
                for t in range(NT):
                    n = tile_rows(t, NCH)
                    bk = ringM.next()
                    for k in range(KD):
                        s = sl[k // 8]
                        svv = wview(s.t, 8, 512)
                        PE(lambda e, bk=bk, k=k, t=t, n=n, svv=svv: e.matmul(
                            bk.t[0:n, :], lhsT=oT.t[:, k, t * 128:t * 128 + n], rhs=svv[:, k % 8, :], start=(k == 0),
                            stop=(k == KD - 1)), [s.r(), oT.r(t)], [bk.r()])
                    accumulate(bk, t, n, cg)
            mod_rows(5)
            for t in range(NT):
                n = tile_rows(t, NCH)
                norm_transpose(x1.t[0:n, t, :], n, a2, oT, t * 128, main_conds(t, n), [x1.r(t)], [oT.r(t)], xnB)
            cut(20)
            P.barrier()

            fsegs = [(0, 1, 0), (1, 1, 1), (2, NCH - 2, 2)]
            wf = ffn_col(NCH - 1) + 64 - 2
            ngroups = (NFF + GFF - 1) // GFF
            for g in range(ngroups):
                f0 = g * GFF
                nf = min(GFF, NFF - f0)
                for fp in range(0, nf, 2):
                    npair = min(2, nf - fp)
                    c0w = (f0 + fp) * 128
                    sg = wload([(lambda t, npair=npair: wview(t, 16, 256, 0, npair * 128), wsrc(w_up, 0, 16, c0w, c0w + npair * 128))])
                    sv_ = wload([(lambda t, npair=npair: wview(t, 16, 256, 0, npair * 128),
                                  wsrc(w_up, 0, 16, DFF + c0w, DFF + c0w + npair * 128))])
                    sgv = wview(sg.t, 16, 256)
                    svv = wview(sv_.t, 16, 256)
                    for j in range(npair):
                        fi = f0 + fp + j
                        lj = fp + j

                        def evac_gate(bk, c0_, n_):
                            for (c, ln, d0) in chunk_runs(c0_, n_, ffn_col):
                                ACT(lambda e, c=c, ln=ln, d0=d0: e.copy(out=gbuf.t[:, d0:d0 + ln], in_=bk.t[:, c - c0_:c - c0_ + ln]),
                                    [bk.r(), gbuf.r()], [gbuf.r()])

                        proj_fm(oT, sg, lambda k, j=j: sgv[:, k, j * 128:(j + 1) * 128], T, evac_gate, 128)
                        for (c_first, n_c, slot) in fsegs:
                            d0 = ffn_col(c_first) - 2
                            if slot == 2:
                                DVE(lambda e, d0=d0: e.memset(gbuf.t[:, d0:d0 + 2], 0.0), [gbuf.r()], [gbuf.r()])
                            else:
                                P.dma("sp", stl.next(), [(gbuf.t[:, d0:d0 + 2], sffn_d[slot, fi])], writes=[gbuf.r()])
                        DVE(lambda e, fi=fi: e.tensor_scalar(
                            out=cacc.t[:, 2:2 + wf], in0=gbuf.t[:, 2:2 + wf], scalar1=fcw.t[:, fi, 2:3],
                            scalar2=fcb.t[:, fi:fi + 1], op0=ALU.mult, op1=ALU.add), [gbuf.r(), RC, cacc.r()], [cacc.r()])
                        for i in range(2):
                            DVE(lambda e, fi=fi, i=i: e.scalar_tensor_tensor(
                                out=cacc.t[:, 2:2 + wf], in0=gbuf.t[:, i:i + wf], scalar=fcw.t[:, fi, i:i + 1],
                                in1=cacc.t[:, 2:2 + wf], op0=ALU.mult, op1=ALU.add), [gbuf.r(), cacc.r(), RC], [cacc.r()])
                        for (c_first, n_c, slot) in fsegs:
                            e0 = ffn_col(c_first + n_c - 1) + 64
                            P.dma("sp", olr.next(), [(offn_d[slot, fi], gbuf.t[:, e0 - 2:e0])], reads=[gbuf.r()])
                        ACT(lambda e: e.activation(out=cacc.t[:, 2:2 + wf], in_=cacc.t[:, 2:2 + wf], func=AF.Silu),
                            [cacc.r()], [cacc.r()])

                        def evac_val(bk, c0_, n_, lj=lj):
                            for (c, ln, d0) in chunk_runs(c0_, n_, ffn_col):
                                DVE(lambda e, c=c, ln=ln, d0=d0: e.tensor_tensor(
                                    out=actT.t[:, lj, c:c + ln], in0=bk.t[:, c - c0_:c - c0_ + ln], in1=cacc.t[:, d0:d0 + ln],
                                    op=ALU.mult), [bk.r(), cacc.r(), actT.r(lj)], [actT.r(lj)])

                        proj_fm(oT, sv_, lambda k, j=j: svv[:, k, j * 128:(j + 1) * 128], T, evac_val, 128)
                for cg in range(4):
                    sd = wload([(lambda t, nf=nf: wview(t, nf, 512), wsrc(w_down, f0 * 128, nf, cg * 512, (cg + 1) * 512))])
                    sdv = wview(sd.t, nf, 512)
                    for t in range(NT):
                        n = tile_rows(t, NCH)
                        bk = ringM.next()
                        for lj in range(nf):
                            PE(lambda e, bk=bk, lj=lj, t=t, n=n, sdv=sdv: e.matmul(
                                bk.t[0:n, :], lhsT=actT.t[:, lj, t * 128:t * 128 + n], rhs=sdv[:, lj, :], start=(lj == 0),
                                stop=(lj == nf - 1)), [sd.r(), actT.r(lj)], [bk.r()])
                        accumulate(bk, t, n, cg)
            cut(30)
            P.barrier()
            fnrow = growb.t[:, 0, :]
            P.dma("sp", lc, [(fnrow, fnorm_row.to_broadcast([128, D]))], writes=[growb.r(0)])
            ylanes = [P.lane("y%d" % i) for i in range(2)]
            for t in range(NT):
                n = tile_rows(t, NCH)
                ACT(lambda e, n=n, t=t: e.activation(out=growb.t[0:n, 1, :], in_=x1.t[0:n, t, :], func=AF.Square,
                                                     accum_out=stat.t[0:n, 6:7]), [x1.r(t)], [growb.r(1), stat.r()])
                rstd_from_ss(stat.t[0:n, 6:7], stat.t[0:n, 7:8], 1.0 / D, [stat.r()])
                DVE(lambda e, n=n, t=t: e.scalar_tensor_tensor(
                    out=x1.t[0:n, t, :], in0=x1.t[0:n, t, :], scalar=stat.t[0:n, 7:8], in1=fnrow[0:n, :], op0=ALU.mult,
                    op1=ALU.mult), [x1.r(t), stat.r(), growb.r(0)], [x1.r(t)])
                P.dma("sp", ylanes[t % 2], [(y_d[t * 128:t * 128 + n, :], x1.t[0:n, t, :])], reads=[x1.r(t)])
        except StopBuild:
            pass
        P.emit()
        build.info = dict(nops=P.nops, nticks=P.nticks)
    return nc


def fm(v, n):
    return np.ascontiguousarray(np.asarray(v, np.float32).reshape(n, 128).T)


def shared_inputs(inp, cfg):
    f = lambda a: np.ascontiguousarray(np.asarray(a, np.float32))
    nff = cfg.NFF
    return {
        "cst": host_consts(),
        "w_ada": f(inp["w_ada"][0]),
        "bada_fm": fm(inp["b_ada"][0], 96),
        "bada_row": f(inp["b_ada"][0]).reshape(1, -1),
        "norm1_fm": fm(inp["norm1"][0], 16),
        "norm2_fm": fm(inp["norm2"][0], 16),
        "fnorm_row": f(inp["final_norm"]).reshape(1, -1),
        "w_in": f(inp["w_in"][0]),
        "gla_wg": f(inp["gla_wg"][0]),
        "gla_bg_fm": fm(inp["gla_bg"][0], 4),
        "gla_norm_fm": fm(inp["gla_norm"][0], 2),
        "gdn_norm_fm": fm(inp["gdn_norm"][0], 1),
        "gdn_convw_fm": np.ascontiguousarray(f(inp["gdn_conv_w"][0]).reshape(4, 24, 128).transpose(2, 1, 0)),
        "alog_row": f(inp["gdn_a_log"][0]).reshape(1, 8),
        "dtb_row": f(inp["gdn_dt_bias"][0]).reshape(1, 8),
        "w_out": f(inp["w_out"][0]),
        "w_up": f(inp["w_up"][0]),
        "ffn_convw_fm": np.ascontiguousarray(f(inp["ffn_conv_w"][0]).reshape(3, nff, 128).transpose(2, 1, 0)),
        "ffn_convb_fm": fm(inp["ffn_conv_b"][0], nff),
        "w_down": f(inp["w_down"][0]),
    }


def core_inputs(cfg, xs2, cs2, xlb, xp, cp, xwarm, flagv, sgla2, sgdn2, sconv2, sffn2):
    f = lambda a: np.ascontiguousarray(np.asarray(a, np.float32))
    nff = cfg.NFF
    xm = np.concatenate([xs2[0], xs2[1], xlb, xp], axis=0)
    crow = np.stack([cs2[0], cs2[1], cp], axis=0)
    return {
        "xm": f(xm),
        "xw": f(xwarm) if cfg.WCH > 0 else np.zeros((64, D), np.float32),
        "cT": np.ascontiguousarray(f(crow).reshape(3, KD, 128).transpose(2, 1, 0)),
        "flag": np.full((128, 1), flagv, np.float32),
        "sgla": f(sgla2),
        "sgdn": f(sgdn2),
        "sconv": np.ascontiguousarray(f(sconv2).reshape(2, 3, 24, 128).transpose(0, 2, 3, 1)),
        "sffn": np.ascontiguousarray(f(sffn2).reshape(2, 2, nff, 128).transpose(0, 2, 3, 1)),
    }


_CACHE = {}


def kernel(**inp):
    cfg = Cfg()
    if "nc" not in _CACHE:
        _CACHE["nc"] = build(cfg)
    nc = _CACHE["nc"]
    xp_, xs_ = np.asarray(inp["x_prompt"]), np.asarray(inp["x_sample"])
    cp_, cs_ = np.asarray(inp["c_prompt"]), np.asarray(inp["c_sample"])
    sh = shared_inputs(inp, cfg)
    in_maps = []
    for c in range(8):
        b, h = c // 2, c % 2
        s0, s1 = 2 * c, 2 * c + 1
        lb0 = 1024 * h - 64 if h else 0
        m = core_inputs(cfg, [xs_[s0], xs_[s1]], [cs_[s0], cs_[s1]], xp_[b, lb0:lb0 + 64],
                        xp_[b, 1024 * h:1024 * h + 1024], cp_[b], xp_[b, 0:960], float(h),
                        np.asarray(inp["state_gla"])[0, s0:s1 + 1], np.asarray(inp["state_gdn"])[0, s0:s1 + 1],
                        np.asarray(inp["state_gdn_conv"])[0, s0:s1 + 1], np.asarray(inp["state_ffn_conv"])[0, s0:s1 + 1])
        m.update(sh)
        in_maps.append(m)
    res = run_bass_kernel_spmd(nc, in_maps, core_ids=list(range(8)))
    r = res.results
    B, S = xp_.shape[0], xp_.shape[1]
    nff = cfg.NFF
    y_p = np.zeros((B, S, D), np.float32)
    y_s = np.zeros(xs_.shape, np.float32)
    p_gla = np.zeros((1, B, 4, 128, 256), np.float32)
    p_gdn = np.zeros((1, B, 8, 128, 128), np.float32)
    p_conv = np.zeros((1, B, 3, 3072), np.float32)
    p_ffn = np.zeros((1, B, 2, 128 * nff), np.float32)
    s_gla = np.zeros((1, 16, 4, 128, 256), np.float32)
    s_gdn = np.zeros((1, 16, 8, 128, 128), np.float32)
    s_conv = np.zeros((1, 16, 3, 3072), np.float32)
    s_ffn = np.zeros((1, 16, 2, 128 * nff), np.float32)
    cv = lambda a: np.asarray(a).transpose(2, 0, 1).reshape(a.shape[2], -1)
    for c in range(8):
        b, h = c // 2, c % 2
        o = r[c]
        y = np.asarray(o["y"])
        y_s[2 * c] = y[0:64]
        y_s[2 * c + 1] = y[64:128]
        y_p[b, 1024 * h:1024 * h + 1024] = y[192:192 + 1024]
        for j in range(2):
            s_gla[0, 2 * c + j] = o["ogla"][j]
            s_gdn[0, 2 * c + j] = o["ogdn"][j]
            s_conv[0, 2 * c + j] = cv(o["oconv"][j])
            s_ffn[0, 2 * c + j] = cv(o["offn"][j])
        if h == 1:
            p_gla[0, b] = o["ogla"][2]
            p_gdn[0, b] = o["ogdn"][2]
            p_conv[0, b] = cv(o["oconv"][2])
            p_ffn[0, b] = cv(o["offn"][2])
    return (y_p, y_s, p_gla, p_gdn, p_conv, p_ffn, s_gla, s_gdn, s_conv, s_ffn)
```
